# Optimizing a Trainium2 kernel written in Bass

```python
import math
import jax, jax.numpy as jnp
from jax import lax
import numpy as np

D_MODEL = 4096
BATCH = 4
SEQ = 2048
DEPTH = 1

N_META = 16
DA_HEADS = D_MODEL // 256
DA_QK_DIM = 64
DA_V_DIM = 2 * DA_QK_DIM
ROPE_DIMS = DA_QK_DIM // 4
ROPE_THETA = 500000.0
Q_BLOCK = 128
HG_HEADS = D_MODEL // 256
HG_K_DIM = 128
HG_V_DIM = 128
HG_CHUNK = 64
DA_QK_WIDTH = DA_HEADS * 2 * DA_QK_DIM
DA_V_WIDTH = DA_HEADS * DA_V_DIM
HG_K_WIDTH = HG_HEADS * HG_K_DIM
HG_V_WIDTH = HG_HEADS * HG_V_DIM
IN_SPLITS = (DA_QK_WIDTH, DA_QK_WIDTH, DA_V_WIDTH,
             HG_K_WIDTH, HG_K_WIDTH, HG_V_WIDTH, HG_V_WIDTH,
             D_MODEL, D_MODEL)
IN_WIDTH = sum(IN_SPLITS)
D_FF = 4 * D_MODEL
EPS = 1e-6
NEG_INF = -1e30

kernel_name = "hybrid_diffattn_hgrn2_gated_block"


def _rmsnorm(x, g):
    xf = x.astype(jnp.float32)
    r = lax.rsqrt(jnp.mean(xf * xf, axis=-1, keepdims=True) + EPS)
    return (xf * r).astype(x.dtype) * g.astype(x.dtype)


def _partial_rope(x, pos):
    half = ROPE_DIMS // 2
    inv_freq = ROPE_THETA ** (-(jnp.arange(half, dtype=jnp.float32) * 2.0) / ROPE_DIMS)
    ang = pos.astype(jnp.float32)[:, None] * inv_freq[None, :]
    cos = jnp.cos(ang)[:, None, None, :].astype(x.dtype)
    sin = jnp.sin(ang)[:, None, None, :].astype(x.dtype)
    x1 = x[..., :half]
    x2 = x[..., half:ROPE_DIMS]
    rest = x[..., ROPE_DIMS:]
    return jnp.concatenate([x1 * cos - x2 * sin, x2 * cos + x1 * sin, rest], axis=-1)


def _diff_attention(q, k, v, q_norm, k_norm, lam_q1, lam_k1, lam_q2, lam_k2, subln, layer_idx):
    B, T = q.shape[0], q.shape[1]
    pad = (-T) % Q_BLOCK
    L = pad + T
    nb = L // Q_BLOCK
    lam_init = 0.8 - 0.6 * math.exp(-0.3 * layer_idx)
    lam = (jnp.exp(jnp.sum(lam_q1.astype(jnp.float32) * lam_k1.astype(jnp.float32)))
           - jnp.exp(jnp.sum(lam_q2.astype(jnp.float32) * lam_k2.astype(jnp.float32)))
           + lam_init)
    q = _rmsnorm(q, q_norm)
    k = _rmsnorm(k, k_norm)
    q = jnp.pad(q, ((0, 0), (pad, 0), (0, 0), (0, 0), (0, 0)))
    k = jnp.pad(k, ((0, 0), (pad, 0), (0, 0), (0, 0), (0, 0)))
    v = jnp.pad(v, ((0, 0), (pad, 0), (0, 0), (0, 0)))
    pos = jnp.arange(L, dtype=jnp.int32) - pad
    q = _partial_rope(q, pos)
    k = _partial_rope(k, pos)
    scale = DA_QK_DIM ** -0.5
    key_pos = jnp.arange(L, dtype=jnp.int32)
    key_valid = key_pos >= pad
    q_blocks = q.reshape(B, nb, Q_BLOCK, DA_HEADS, 2, DA_QK_DIM).transpose(1, 0, 2, 3, 4, 5)

    def block(args):
        qb, bi = args
        s = jnp.einsum('bqhcd,bkhcd->bhcqk', qb, k,
                       preferred_element_type=jnp.float32) * scale
        qpos = bi * Q_BLOCK + jnp.arange(Q_BLOCK, dtype=jnp.int32)
        mask = (key_pos[None, :] <= qpos[:, None]) & key_valid[None, :]
        s = jnp.where(mask, s, NEG_INF)
        p = jax.nn.softmax(s, axis=-1)
        a = p[:, :, 0] - lam * p[:, :, 1]
        return jnp.einsum('bhqk,bkhe->bqhe', a.astype(v.dtype), v)

    out = lax.map(block, (q_blocks, jnp.arange(nb, dtype=jnp.int32)))
    out = out.transpose(1, 0, 2, 3, 4).reshape(B, L, DA_HEADS, DA_V_DIM)[:, pad:]
    out = _rmsnorm(out, subln) * (1.0 - lam_init)
    return out.reshape(B, T, DA_V_WIDTH)


def _hgrn2(q, f_logit, i, gate, lower_bound, out_norm, layer_idx):
    B, T = q.shape[0], q.shape[1]
    dt = q.dtype
    C = HG_CHUNK
    pad = (-T) % C
    L = pad + T
    nc = L // C
    lb_all = jnp.cumsum(jax.nn.softmax(lower_bound.astype(jnp.float32), axis=0), axis=0)
    lb = lb_all[layer_idx].reshape(HG_HEADS, HG_K_DIM)
    f = lb + (1.0 - lb) * jax.nn.sigmoid(f_logit.astype(jnp.float32))
    g = jnp.log(f)
    kk = 1.0 - f
    qf = q.astype(jnp.float32)
    vf = i.astype(jnp.float32)

    def to_chunks(a):
        a = jnp.pad(a, ((0, 0), (pad, 0), (0, 0), (0, 0)))
        return a.reshape(B, nc, C, HG_HEADS, a.shape[-1]).transpose(0, 3, 1, 2, 4)

    qc, gc, kc, vc = to_chunks(qf), to_chunks(g), to_chunks(kk), to_chunks(vf)
    b = jnp.cumsum(gc, axis=3)
    b_ref = b[:, :, :, C // 2 - 1:C // 2]
    b_last = b[:, :, :, C - 1:C]
    q_in = qc * jnp.exp(b - b_ref)
    k_in = kc * jnp.exp(b_ref - b)
    A = jnp.einsum('bhntd,bhnsd->bhnts', q_in, k_in)
    causal = jnp.tril(jnp.ones((C, C), dtype=bool))
    A = jnp.where(causal, A, 0.0)
    o_intra = jnp.einsum('bhnts,bhnsv->bhntv', A, vc)
    k_dec = kc * jnp.exp(b_last - b)
    U = jnp.einsum('bhnsd,bhnsv->bhndv', k_dec, vc)
    decay = jnp.exp(b_last[:, :, :, 0])

    def step(S, inp):
        U_c, d_c = inp
        return d_c[..., None] * S + U_c, S

    S0 = jnp.zeros((B, HG_HEADS, HG_K_DIM, HG_V_DIM), jnp.float32)
    _, S_start = lax.scan(step, S0, (U.transpose(2, 0, 1, 3, 4), decay.transpose(2, 0, 1, 3)))
    S_start = S_start.transpose(1, 2, 0, 3, 4)
    o_inter = jnp.einsum('bhntd,bhndv->bhntv', qc * jnp.exp(b), S_start)
    o = (o_intra + o_inter).transpose(0, 2, 3, 1, 4).reshape(B, L, HG_HEADS, HG_V_DIM)[:, pad:]
    o = _rmsnorm(o, out_norm.astype(jnp.float32)) * jax.nn.sigmoid(gate.astype(jnp.float32))
    return o.astype(dt).reshape(B, T, HG_V_WIDTH)


def setup_inputs(seed: int = 0) -> dict:
    key = jax.random.key(seed)
    ks = jax.random.split(key, 20)
    nrm = jax.random.normal
    f32 = jnp.float32
    return {
        "x": nrm(ks[0], (BATCH, SEQ, D_MODEL), f32),
        "meta_tokens": nrm(ks[1], (N_META, D_MODEL), f32),
        "norm_mix": 1.0 + 0.02 * nrm(ks[2], (DEPTH, D_MODEL), f32),
        "w_in": nrm(ks[3], (DEPTH, D_MODEL, IN_WIDTH), f32) * D_MODEL ** -0.5,
        "da_q_norm": 1.0 + 0.02 * nrm(ks[4], (DEPTH, DA_QK_DIM), f32),
        "da_k_norm": 1.0 + 0.02 * nrm(ks[5], (DEPTH, DA_QK_DIM), f32),
        "da_lambda_q1": 0.1 * nrm(ks[6], (DEPTH, DA_QK_DIM), f32),
        "da_lambda_k1": 0.1 * nrm(ks[7], (DEPTH, DA_QK_DIM), f32),
        "da_lambda_q2": 0.1 * nrm(ks[8], (DEPTH, DA_QK_DIM), f32),
        "da_lambda_k2": 0.1 * nrm(ks[9], (DEPTH, DA_QK_DIM), f32),
        "da_subln": 1.0 + 0.02 * nrm(ks[10], (DEPTH, DA_V_DIM), f32),
        "hg_lower_bound": 0.1 * nrm(ks[11], (DEPTH + 1, HG_K_WIDTH), f32),
        "hg_out_norm": 1.0 + 0.02 * nrm(ks[12], (DEPTH, HG_V_DIM), f32),
        "w_up_a": nrm(ks[13], (DEPTH, DA_V_WIDTH, D_MODEL), f32) * DA_V_WIDTH ** -0.5,
        "w_up_b": nrm(ks[14], (DEPTH, HG_V_WIDTH, D_MODEL), f32) * HG_V_WIDTH ** -0.5,
        "w_out": nrm(ks[15], (DEPTH, D_MODEL, D_MODEL), f32) * D_MODEL ** -0.5,
        "norm_mlp": 1.0 + 0.02 * nrm(ks[16], (DEPTH, D_MODEL), f32),
        "w_ff1": nrm(ks[17], (DEPTH, D_MODEL, D_FF), f32) * D_MODEL ** -0.5,
        "w_ff2": nrm(ks[18], (DEPTH, D_FF, D_MODEL), f32) * D_FF ** -0.5,
    }


def reference(x, meta_tokens, norm_mix, w_in, da_q_norm, da_k_norm, da_lambda_q1, da_lambda_k1,
              da_lambda_q2, da_lambda_k2, da_subln, hg_lower_bound, hg_out_norm, w_up_a, w_up_b,
              w_out, norm_mlp, w_ff1, w_ff2):
    B = x.shape[0]
    meta = jnp.broadcast_to(meta_tokens.astype(x.dtype)[None], (B, N_META, D_MODEL))
    h = jnp.concatenate([meta, x], axis=1)
    T = h.shape[1]
    split_points = tuple(int(s) for s in np.cumsum(IN_SPLITS)[:-1])
    for l in range(DEPTH):
        u = _rmsnorm(h, norm_mix[l])
        proj = jnp.einsum('btd,de->bte', u, w_in[l])
        da_q, da_k, da_v, hg_q, hg_f, hg_i, hg_g, gate_a, gate_b = jnp.split(proj, split_points, axis=-1)
        y_a = _diff_attention(
            da_q.reshape(B, T, DA_HEADS, 2, DA_QK_DIM),
            da_k.reshape(B, T, DA_HEADS, 2, DA_QK_DIM),
            da_v.reshape(B, T, DA_HEADS, DA_V_DIM),
            da_q_norm[l], da_k_norm[l], da_lambda_q1[l], da_lambda_k1[l],
            da_lambda_q2[l], da_lambda_k2[l], da_subln[l], l)
        y_b = _hgrn2(
            hg_q.reshape(B, T, HG_HEADS, HG_K_DIM),
            hg_f.reshape(B, T, HG_HEADS, HG_K_DIM),
            hg_i.reshape(B, T, HG_HEADS, HG_V_DIM),
            hg_g.reshape(B, T, HG_HEADS, HG_V_DIM),
            hg_lower_bound, hg_out_norm[l], l)
        y_a = jnp.einsum('bte,ed->btd', y_a, w_up_a[l])
        y_b = jnp.einsum('bte,ed->btd', y_b, w_up_b[l])
        merged = jax.nn.sigmoid(gate_a) * y_a + jax.nn.sigmoid(gate_b) * y_b
        h = h + jnp.einsum('btd,de->bte', merged, w_out[l])
        v = _rmsnorm(h, norm_mlp[l])
        hid = jnp.square(jax.nn.relu(jnp.einsum('btd,df->btf', v, w_ff1[l])))
        h = h + jnp.einsum('btf,fd->btd', hid, w_ff2[l])
    return h[:, N_META:]
```

```python
import contextlib
import math
import numpy as np
import concourse.bass as bass
import concourse.mybir as mybir
from concourse.bass_utils import run_bass_kernel_spmd

F32 = mybir.dt.float32
BF16 = mybir.dt.bfloat16
AF = mybir.ActivationFunctionType
ALU = mybir.AluOpType
AX = mybir.AxisListType

EPS = 1e-6
N_META = 16
ROPE_THETA = 500000.0


class _Op:
    __slots__ = ("eng", "fn", "deps", "is_dma", "sem", "val", "needs_inc", "count")

    def __init__(self, eng, fn, is_dma=False, sem=None):
        self.eng = eng
        self.fn = fn
        self.deps = []
        self.is_dma = is_dma
        self.sem = sem
        self.val = 0
        self.needs_inc = False
        self.count = 0


class Prog:
    CENG = ("pe", "act", "dve", "pool")
    ALLENG = ("pe", "act", "dve", "pool", "sp")

    def __init__(self, nc, n_dma_sems=20):
        self.nc = nc
        self.ops = []
        self.last_w = {}
        self.readers = {}
        self.n_dma_sems = n_dma_sems
        self.dma_last = [None] * n_dma_sems
        self.dma_val = [0] * n_dma_sems
        self.last_on = {}
        self.bar_ops = []
        self.bar_pending = set()

    def barrier(self):
        self.bar_ops = [o for o in self.last_on.values()] + [o for o in self.dma_last if o is not None]
        self.bar_pending = set(self.ALLENG)
        self.last_w = {}
        self.readers = {}

    def _track(self, op, reads, writes):
        deps = {}
        for r in reads:
            w = self.last_w.get(r)
            if w is not None:
                deps[id(w)] = w
            if isinstance(r, str) and r.startswith("ps"):
                for ek, rd in self.readers.get(r, {}).items():
                    if ek != op.eng:
                        deps[id(rd)] = rd
        for wk in writes:
            w = self.last_w.get(wk)
            if w is not None:
                deps[id(w)] = w
            for rd in self.readers.get(wk, {}).values():
                deps[id(rd)] = rd
        if op.eng in self.bar_pending:
            self.bar_pending.discard(op.eng)
            for b in self.bar_ops:
                deps[id(b)] = b
        for r in reads:
            d = self.readers.setdefault(r, {})
            d[("d", op.sem) if op.is_dma else op.eng] = op
        for wk in writes:
            self.last_w[wk] = op
            self.readers[wk] = {}
        deps.pop(id(op), None)
        op.deps = list(deps.values())
        self.ops.append(op)
        self.last_on[op.eng] = op
        return op

    def op(self, eng, fn, reads=(), writes=()):
        return self._track(_Op(eng, fn), reads, writes)

    def dma(self, queue, sem, fn, reads=(), writes=()):
        op = _Op(queue, fn, is_dma=True, sem=sem)
        self._track(op, reads, writes)
        prev = self.dma_last[sem]
        if prev is not None and all(prev is not d for d in op.deps):
            op.deps.append(prev)
        self.dma_last[sem] = op
        self.dma_val[sem] += 16
        op.val = self.dma_val[sem]
        return op

    def emit(self, final_eng="sp"):
        nc = self.nc
        for op in self.ops:
            for d in op.deps:
                if not d.is_dma:
                    if d.eng == "pe" and op.eng == "pe" and not op.is_dma:
                        continue
                    d.needs_inc = True
        counts = {e: 0 for e in self.CENG}
        for op in self.ops:
            if not op.is_dma and op.needs_inc:
                counts[op.eng] += 1
                op.count = counts[op.eng]
        streams = {e: [] for e in self.ALLENG}
        for op in self.ops:
            streams[op.eng].append(op)
        with contextlib.ExitStack() as st:
            esem = {e: st.enter_context(nc.semaphore("es_" + e)) for e in self.CENG}
            dsem = [st.enter_context(nc.semaphore("ds_%d" % i)) for i in range(self.n_dma_sems)]
            block = st.enter_context(nc.Block())

            def run(ename, eng):
                waited = {}
                for op in streams[ename]:
                    need = {}
                    for d in op.deps:
                        if d.is_dma:
                            k = ("d", d.sem)
                            v = d.val
                        else:
                            if d.eng == "pe" and ename == "pe" and not op.is_dma:
                                continue
                            k = ("e", d.eng)
                            v = d.count
                        if need.get(k, 0) < v:
                            need[k] = v
                    for k, v in need.items():
                        if waited.get(k, 0) >= v:
                            continue
                        waited[k] = v
                        s = dsem[k[1]] if k[0] == "d" else esem[k[1]]
                        eng.wait_ge(s, v)
                    ins = op.fn(eng)
                    if op.is_dma:
                        ins.then_inc(dsem[op.sem], 16)
                    elif op.needs_inc:
                        ins.then_inc(esem[op.eng], 1)
                if ename == final_eng:
                    for i in range(self.n_dma_sems):
                        if self.dma_val[i] > 0 and waited.get(("d", i), 0) < self.dma_val[i]:
                            eng.wait_ge(dsem[i], self.dma_val[i])
                    for e in self.CENG:
                        if counts[e] > 0 and waited.get(("e", e), 0) < counts[e]:
                            eng.wait_ge(esem[e], counts[e])

            @block.tensor
            def _(e):
                run("pe", e)

            @block.scalar
            def _(e):
                run("act", e)

            @block.vector
            def _(e):
                run("dve", e)

            @block.gpsimd
            def _(e):
                run("pool", e)

            @block.sync
            def _(e):
                run("sp", e)


class Cfg:
    def __init__(self, D, SEQ, DFF, B=4):
        self.D = D
        self.SEQ = SEQ
        self.DFF = DFF
        self.B = B
        self.H = D // 256
        self.NCH = D // 128
        self.QK = self.H * 128
        self.INW = 7 * self.QK + 2 * D
        self.NTO = SEQ // 2 // 128
        self.NTP = self.NTO + 1
        self.NT = self.NTO + self.NTP
        self.TO = self.NTO * 128
        self.TP = self.NTP * 128
        self.TL = self.NT * 128
        self.FB = min(2048, DFF)
        self.NSL = D // 512


CM_IDENT, CM_PROT, CM_UT, CM_LT, CM_DM, CM_BO, CM_MTRI, CM_ONES, CM_IND, CM_N = range(10)


def _const_mats():
    m = np.zeros((CM_N, 128, 128), np.float32)
    m[CM_IDENT] = np.eye(128, dtype=np.float32)
    for mm in range(128):
        r = mm % 64
        if r < 8:
            m[CM_PROT][mm + 8, mm] = -1.0
        elif r < 16:
            m[CM_PROT][mm - 8, mm] = 1.0
    s = np.arange(128)[:, None]
    t = np.arange(128)[None, :]
    same = (s // 64) == (t // 64)
    m[CM_UT] = (same & (s > t)).astype(np.float32)
    m[CM_LT] = (same & (s <= t)).astype(np.float32)
    ref = (t // 64) * 64 + 31
    m[CM_DM] = m[CM_LT] - (same & (s <= ref)).astype(np.float32)
    m[CM_BO] = (same).astype(np.float32) / 64.0
    m[CM_MTRI] = (s <= t).astype(np.float32)
    m[CM_ONES] = 1.0
    m[CM_IND][:64, 0] = 1.0
    m[CM_IND][64:, 1] = 1.0
    return np.ascontiguousarray(m.transpose(1, 0, 2).reshape(128, CM_N * 128))


def _rope_tables(pos):
    half = 8
    inv_freq = (np.float32(ROPE_THETA) ** (-(np.arange(half, dtype=np.float32) * np.float32(2.0)) / np.float32(16))).astype(np.float32)
    ang = pos.astype(np.float32)[None, :] * inv_freq[:, None]
    c = np.cos(ang).astype(np.float32)
    s = np.sin(ang).astype(np.float32)
    TL = pos.shape[0]
    cosT = np.ones((128, TL), np.float32)
    sinT = np.zeros((128, TL), np.float32)
    for mp in range(2):
        b = mp * 64
        cosT[b:b + 8] = c
        cosT[b + 8:b + 16] = c
        sinT[b:b + 8] = s
        sinT[b + 8:b + 16] = s
    return cosT, sinT


def build_program(cfg):
    D, H, NCH, QK, DFF = cfg.D, cfg.H, cfg.NCH, cfg.QK, cfg.DFF
    NTO, NTP, NT, TO, TP, TL = cfg.NTO, cfg.NTP, cfg.NT, cfg.TO, cfg.TP, cfg.TL
    FB, NSL = cfg.FB, cfg.NSL
    NFC = FB // 128
    NFB = DFF // FB

    nc = bass.Bass("TRN2", target_bir_lowering=False)

    def din(name, shape, dt=F32):
        return nc.dram_tensor(name, list(shape), dt, kind="ExternalInput").ap()

    xloc = din("xloc", [TL, D])
    w_in = din("w_in", [D, cfg.INW])
    w_up_a = din("w_up_a", [QK, D])
    w_up_b = din("w_up_b", [QK, D])
    w_out = din("w_out", [D, D])
    w_ff1 = din("w_ff1", [D, DFF])
    w_ff2 = din("w_ff2", [DFF, D])
    cmat = din("cmat", [128, CM_N * 128])
    cosT_d = din("cosT", [128, TL])
    sinT_d = din("sinT", [128, TL])
    SV_GMIX = 0
    SV_GMLP = NCH
    SV_QG = 2 * NCH
    SV_KG = 2 * NCH + 1
    SV_LB0 = 2 * NCH + 2
    SV_LB1 = SV_LB0 + H
    SV_VALID = SV_LB1 + H
    SV_SUBLN = SV_VALID + NT
    SV_N = SV_SUBLN + 1
    svec_d = din("svec", [128, SV_N])
    RV_LAM = 0
    RV_SUBLN = 256
    RV_ONORM = 384
    RV_N = 512
    rvec_d = din("rvec", [128, RV_N])
    lbrow_d = din("lbrow", [128, 2, QK])
    out = nc.dram_tensor("out", [TO, D], F32, kind="ExternalOutput").ap()

    def dscr(name, shape, dt=BF16):
        return nc.dram_tensor(name, list(shape), dt, kind="Internal").ap()

    KT_s = dscr("KT_s", [H, 128, TP])
    V_s = dscr("V_s", [H, 128, NTP * 130])
    YA_s = dscr("YA_s", [H, 128, TO])
    YB_s = dscr("YB_s", [H, 128, TO])
    MT_s = dscr("MT_s", [NCH, 128, TO])

    P = Prog(nc, n_dma_sems=20)
    S_SETUP = 0
    S_W = [1, 2, 3, 4]
    S_X = [5, 6]
    S_SCR = [7, 8, 9]
    S_OUT = [10, 11]
    S_ACC = [12, 13, 14, 15]
    S_W2 = [16, 17, 18, 19]
    wctr = [0]

    def wsem():
        wctr[0] += 1
        return S_W[wctr[0] % 4]

    w2ctr = [0]

    def w2sem():
        w2ctr[0] += 1
        return S_W2[w2ctr[0] % 4]

    xctr = [0]

    def xsem():
        xctr[0] += 1
        return S_X[xctr[0] % 2]

    sctr = [0]

    def ssem():
        sctr[0] += 1
        return S_SCR[sctr[0] % 3]

    octr = [0]

    def osem():
        octr[0] += 1
        return S_OUT[octr[0] % 2]

    def mm(out_, lhsT, rhs, start, stop, reads, writes):
        return P.op("pe", lambda e: e.matmul(out_, lhsT=lhsT, rhs=rhs, start=start, stop=stop), reads, writes)

    def tr(out_, in_, ident, reads, writes):
        return P.op("pe", lambda e: e.transpose(out=out_, in_=in_, identity=ident), reads, writes)

    def act(out_, in_, func, reads, writes, scale=None, bias=None, accum=None):
        kw = {}
        if scale is not None:
            kw["scale"] = scale
        if bias is not None:
            kw["bias"] = bias
        if accum is not None:
            kw["accum_out"] = accum
        return P.op("act", lambda e: e.activation(out=out_, in_=in_, func=func, **kw), reads, writes)

    def tt(eng, out_, in0, in1, op, reads, writes):
        return P.op(eng, lambda e: e.tensor_tensor(out=out_, in0=in0, in1=in1, op=op), reads, writes)

    def ts(eng, out_, in0, s1, s2, op0, op1, reads, writes):
        if op1 is None:
            return P.op(eng, lambda e: e.tensor_scalar(out=out_, in0=in0, scalar1=s1, scalar2=None, op0=op0), reads, writes)
        return P.op(eng, lambda e: e.tensor_scalar(out=out_, in0=in0, scalar1=s1, scalar2=s2, op0=op0, op1=op1), reads, writes)

    def stt(eng, out_, in0, scalar, in1, op0, op1, reads, writes):
        return P.op(eng, lambda e: e.scalar_tensor_tensor(out=out_, in0=in0, scalar=scalar, in1=in1, op0=op0, op1=op1),
                    reads, writes)

    def cp(eng, out_, in_, reads, writes):
        if eng == "act":
            return act(out_, in_, AF.Copy, reads, writes)
        return P.op(eng, lambda e: e.tensor_copy(out=out_, in_=in_), reads, writes)

    def amul(out_, in_, mul_ap, reads, writes):
        return P.op("act", lambda e: e.mul(out=out_, in_=in_, mul=mul_ap), reads, writes)

    def recip(out_, in_, reads, writes):
        return P.op("dve", lambda e: e.reciprocal(out=out_, in_=in_), reads, writes)

    def memset(eng, ap, val, writes):
        return P.op(eng, lambda e: e.memset(ap, val), (), writes)

    def dma(queue, sem, out_, in_, reads, writes, accum=False):
        if accum:
            return P.dma(queue, sem, lambda e: e.dma_start(out=out_, in_=in_, accum_op=ALU.add), reads, writes)
        return P.dma(queue, sem, lambda e: e.dma_start(out=out_, in_=in_), reads, writes)

    with contextlib.ExitStack() as st:
        ARENA_WORDS = 52992
        arena = st.enter_context(nc.sbuf_tensor("arena", [128, ARENA_WORDS], F32))
        ps = [st.enter_context(nc.psum_tensor("ps%d" % i, [128, 512], F32)) for i in range(8)]
        psk = ["ps%d" % i for i in range(8)]

        class Alloc:
            def __init__(self, base, limit):
                self.off = base
                self.limit = limit

            def get(self, free_shape, dt):
                n = 1
                for s_ in free_shape:
                    n *= s_
                words = n if dt == F32 else (n + 1) // 2
                words = (words + 7) // 8 * 8
                a0 = self.off
                self.off += words
                assert self.off <= self.limit, ("SBUF arena overflow", self.off, self.limit)
                v = arena[:, a0:a0 + words]
                if dt != F32:
                    v = v.bitcast(dt)
                v = v[:, 0:n]
                if len(free_shape) == 2:
                    v = v.rearrange("p (a b) -> p a b", b=free_shape[1])
                elif len(free_shape) == 3:
                    v = v.rearrange("p (a b c) -> p a b c", b=free_shape[1], c=free_shape[2])
                return v

        A0 = Alloc(0, ARENA_WORDS)
        cm = A0.get([CM_N, 128], F32)
        svec = A0.get([SV_N], F32)
        rvec = A0.get([RV_N], F32)
        ident_bf = A0.get([128], BF16)
        bo_bf = A0.get([128], BF16)
        mtri_bf = A0.get([128], BF16)
        lt_bf = A0.get([128], BF16)
        ones_bf = A0.get([128], BF16)
        mean_bf = A0.get([128], BF16)
        vbias = A0.get([NT], F32)
        misc = A0.get([16], F32)
        omlT = A0.get([H], F32)
        ssq = A0.get([NTO, NSL], F32)
        lamtmp = A0.get([2, 64], F32)
        UT_BASE2 = A0.off
        S_all = A0.get([H, 128], F32)
        UT_BASE = A0.off

        def uview(ntok, late=False):
            base = UT_BASE2 if late else UT_BASE
            w = NCH * ntok // 2
            v = arena[:, base:base + w].bitcast(BF16)
            return v.rearrange("p (a b) -> p a b", b=ntok), base + w

        identf = cm[:, CM_IDENT, :]
        protf = cm[:, CM_PROT, :]
        utf = cm[:, CM_UT, :]
        ltf = cm[:, CM_LT, :]
        dmf = cm[:, CM_DM, :]
        onesf = cm[:, CM_ONES, :]
        C_EPS, C_ONE, C_LAM, C_NLAM = 0, 1, 2, 3
        epsc = misc[:, C_EPS:C_EPS + 1]
        onec = misc[:, C_ONE:C_ONE + 1]
        nlamc = misc[:, C_NLAM:C_NLAM + 1]

        dma("sp", S_SETUP, cm.rearrange("p a b -> p (a b)"), cmat, [], ["cm"])
        dma("sp", S_SETUP, svec, svec_d, [], ["svec"])
        dma("sp", S_SETUP, rvec, rvec_d, [], ["rvec"])
        cp("dve", ident_bf, identf, ["cm"], ["ident_bf"])
        cp("dve", bo_bf, cm[:, CM_BO, :], ["cm"], ["bo_bf"])
        cp("dve", mtri_bf, cm[:, CM_MTRI, :], ["cm"], ["mtri_bf"])
        cp("dve", lt_bf, ltf, ["cm"], ["lt_bf"])
        cp("dve", ones_bf, cm[:, CM_ONES, :], ["cm"], ["ones_bf"])
        ts("dve", mean_bf, cm[:, CM_ONES, :], 1.0 / 128, None, ALU.mult, None, ["cm"], ["mean_bf"])
        ts("dve", vbias, svec[:, SV_VALID:SV_VALID + NT], -1.0, 30000.0, ALU.add, ALU.mult, ["svec"], ["vbias"])
        ts("dve", svec[:, SV_SUBLN:SV_SUBLN + 1], svec[:, SV_SUBLN:SV_SUBLN + 1], 0.8, None, ALU.mult, None, ["svec"], ["svec"])
        memset("dve", epsc, EPS, ["misc"])
        memset("dve", onec, 1.0, ["misc"])
        memset("dve", S_all, 0.0, ["S"])
        tt("dve", lamtmp, rvec[:, RV_LAM:RV_LAM + 128].rearrange("p (a b) -> p a b", b=64),
           rvec[:, RV_LAM + 128:RV_LAM + 256].rearrange("p (a b) -> p a b", b=64), ALU.mult, ["rvec"], ["lamtmp"])
        P.op("dve", lambda e: e.reduce_sum(out=misc[:, 4:6], in_=lamtmp, axis=AX.X), ["lamtmp"], ["misc"])
        act(misc[:, 6:8], misc[:, 4:6], AF.Exp, ["misc"], ["misc"])
        tt("dve", misc[:, C_LAM:C_LAM + 1], misc[:, 6:7], misc[:, 7:8], ALU.subtract, ["misc"], ["misc"])
        ts("dve", misc[:, C_LAM:C_LAM + 1], misc[:, C_LAM:C_LAM + 1], 0.2, None, ALU.add, None, ["misc"], ["misc"])
        ts("dve", nlamc, misc[:, C_LAM:C_LAM + 1], -1.0, None, ALU.mult, None, ["misc"], ["misc"])
        ts("dve", rvec[:, RV_SUBLN:RV_SUBLN + 128], rvec[:, RV_SUBLN:RV_SUBLN + 128], 0.8, None, ALU.mult, None,
           ["rvec"], ["rvec"])
        tt("dve", omlT, svec[:, SV_LB0:SV_LB0 + H], svec[:, SV_LB1:SV_LB1 + H], ALU.subtract, ["svec"], ["omlT"])
        act(omlT, omlT, AF.Exp, ["omlT"], ["omlT"])
        ts("dve", omlT, omlT, 1.0, None, ALU.add, None, ["omlT"], ["omlT"])
        recip(omlT, omlT, ["omlT"], ["omlT"])

        w_in_v = w_in.rearrange("(c p) e -> p c e", p=128)

        class WStream:
            def __init__(self, bufs, semfn):
                self.bufs = bufs
                self.n = len(bufs)
                self.specs = []
                self.next = 0
                self.semfn = semfn

            def add(self, src, nchunk):
                self.specs.append((src, nchunk))
                return len(self.specs) - 1

            def advance(self, a):
                while self.next < min(len(self.specs), a + self.n):
                    i = self.next
                    src, nchunk = self.specs[i]
                    u = i % self.n
                    dma("pool", self.semfn(), self.bufs[u][:, 0:nchunk, :], src, [], [("wu", u)])
                    self.next += 1

            def buf(self, i):
                return self.bufs[i % self.n], ("wu", i % self.n)

        def groups(ntok):
            g = []
            o = 0
            while o < ntok:
                s_ = min(512, ntok - o)
                g.append((o, s_))
                o += s_
            return g

        WBUF_WORDS = NCH * 896 // 2

        def phase_norm(tile0, ntile):
            uT, PH0 = uview(ntile * 128)
            A = Alloc(PH0 + WBUF_WORDS, ARENA_WORDS)
            xb = [A.get([D], F32) for _ in range(2)]
            xn = [A.get([D], BF16) for _ in range(2)]
            st1 = A.get([8], F32)
            x4 = [S_X[0], S_X[1], S_OUT[0], S_OUT[1]]
            hD = D // 2
            for ti in range(ntile):
                b = ti % 2
                kx, kn = "xb%d" % b, "xn%d" % b
                r0 = (tile0 + ti) * 128
                dma("sp", x4[2 * b], xb[b][:, 0:hD], xloc[r0:r0 + 128, 0:hD], [], [kx + "a"])
                dma("sp", x4[2 * b + 1], xb[b][:, hD:D], xloc[r0:r0 + 128, hD:D], [], [kx + "b"])
                memset("dve", st1[:, 0:1], 0.0, ["st1"])
                act(xn[b], xb[b], AF.Square, [kx + "a", kx + "b"], [kn, "st1"], accum=st1[:, 0:1])
                ts("dve", st1[:, 1:2], st1[:, 0:1], 1.0 / D, EPS, ALU.mult, ALU.add, ["st1"], ["st1"])
                act(st1[:, 2:3], st1[:, 1:2], AF.Ln, ["st1"], ["st1"])
                act(st1[:, 3:4], st1[:, 2:3], AF.Exp, ["st1"], ["st1"], scale=-0.5)
                ts("dve", xn[b], xb[b], st1[:, 3:4], None, ALU.mult, None, [kx + "a", kx + "b", "st1"], [kn])
                for c8 in range(0, NCH, 8):
                    bk = (c8 // 8) % 2
                    pst = ps[bk][:].bitcast(BF16)
                    n8 = min(8, NCH - c8)
                    for i in range(n8):
                        c = c8 + i
                        tr(pst[:, i * 128:(i + 1) * 128], xn[b][:, c * 128:(c + 1) * 128], ident_bf,
                           [kn, "ident_bf"], [psk[bk]])
                    for i in range(n8):
                        c = c8 + i
                        o_ = uT[:, c, ti * 128:(ti + 1) * 128]
                        gcol = svec[:, SV_GMIX + c:SV_GMIX + c + 1]
                        if bk == 0:
                            ts("dve", o_, pst[:, i * 128:(i + 1) * 128], gcol, None, ALU.mult, None,
                               [psk[bk], "svec"], [("uT", ti, c)])
                        else:
                            amul(o_, pst[:, i * 128:(i + 1) * 128], gcol, [psk[bk], "svec"], [("uT", ti, c)])

        def phase_mixer(own, prefetch_only=False):
            ntile = NTO if own else NTP
            ntok = ntile * 128
            tok0 = TP if own else 0
            tile0 = NTP if own else 0
            uT, PH0 = uview(ntok)
            A = Alloc(PH0, ARENA_WORDS)
            wbuf = A.get([NCH, 896], BF16)
            cosT = A.get([ntok], F32)
            sinT = A.get([ntok], F32)
            KT = A.get([TL], BF16)
            Vaug = A.get([NT, 130], BF16)
            if own:
                QTm = [A.get([TO], BF16) for _ in range(2)]
                hqT = A.get([TO], F32)
                yaT = A.get([TO], BF16)
                ybT = A.get([TO], BF16)
            omlrow = A.get([2, 128], F32)
            sqb = [A.get([512], BF16) for _ in range(2)]
            zc = [A.get([512], F32) for _ in range(2)]
            rst = [A.get([512], F32)] * 2
            qg = [A.get([512], F32) for _ in range(2)]
            rb = [A.get([512], F32)] * 2
            RA, RC = 4, 3
            g_t = [A.get([128], F32) for _ in range(RA)]
            kk_t = [A.get([128], F32) for _ in range(RA)]
            vi_t = [A.get([128], BF16) for _ in range(RA)]
            kdec_t = [A.get([2, 128], BF16) for _ in range(RC)]
            etmp = [A.get([128], F32) for _ in range(2)]
            dec_t = [A.get([2], F32) for _ in range(RC)]
            if own:
                PT = [A.get([512], BF16) for _ in range(3)]
                gate_t = [A.get([128], F32) for _ in range(RA)]
                e1_t = [A.get([3, 128], F32) for _ in range(2)]
                qin_t = [A.get([3, 128], BF16) for _ in range(RC)]
                atm = [A.get([128], BF16) for _ in range(2)]
                sbf = [A.get([2, 128], BF16) for _ in range(2)]
                osb = [A.get([128], F32) for _ in range(2)]
                otmp = [A.get([128], F32) for _ in range(2)]
                ybt = [A.get([128], BF16) for _ in range(2)]
                hst = [A.get([8], F32) for _ in range(2)]

            if not prefetch_only:
                dma("sp", S_SETUP, cosT, cosT_d[:, tok0:tok0 + ntok], [], ["cosT"])
                dma("sp", S_SETUP, sinT, sinT_d[:, tok0:tok0 + ntok], [], ["sinT"])
                memset("pool", Vaug, 0.0, ["Vaug"])
                if own:
                    memset("pool", QTm[0][64:128, :], 0.0, ["QTz"])
                    memset("pool", QTm[1][0:64, :], 0.0, ["QTz"])

            BQ, BK, BHQ, BHF, BV, BHI, BHG = range(7)
            colbase = {BQ: 0, BK: QK, BV: 2 * QK, BHQ: 3 * QK, BHF: 4 * QK, BHI: 5 * QK, BHG: 6 * QK}
            blocks = [BQ, BK, BHQ, BHF, BV, BHI, BHG] if own else [BK, BHF, BV, BHI]
            fonly = [bb for bb in blocks if bb < BHF]
            tblk = [bb for bb in blocks if bb >= BHF]
            ntc = 512 if own else 384
            grp = groups(ntok)
            pctr = [0]

            class Defer:
                def __init__(self):
                    self.q = []
                    self.t = 0

                def later(self, delay, fn):
                    self.q.append((self.t + delay, fn))

                def tick(self):
                    self.t += 1
                    due = [f for (d_, f) in self.q if d_ <= self.t]
                    self.q = [(d_, f) for (d_, f) in self.q if d_ > self.t]
                    for f in due:
                        f()

                def flush(self):
                    while self.q:
                        self.tick()

            DF = Defer()
            in_attn = [False]
            early = []

            def run_early():
                fs = list(early)
                del early[:]
                for f in fs:
                    f()

            cur_h = [0]

            def slot(bidx, h):
                if own:
                    return bidx
                if bidx == BK:
                    return 6
                return {BHF: 0, BV: 1, BHI: 2}[bidx] + (0 if h % 2 == 0 else 3)

            def load_w(h, which):
                for bidx in which:
                    col = colbase[bidx] + h * 128
                    sl = slot(bidx, h)
                    dma("pool", wsem(), wbuf[:, :, sl * 128:(sl + 1) * 128], w_in_v[:, :, col:col + 128],
                        [], [("wb", sl)])

            def fproj(bidx, g0, gs):
                bk = pctr[0] % 2
                pctr[0] += 1
                sl = slot(bidx, cur_h[0])
                for c in range(NCH):
                    mm(ps[bk][:, 0:gs], wbuf[:, c, sl * 128:(sl + 1) * 128], uT[:, c, g0:g0 + gs],
                       c == 0, c == NCH - 1, [("wb", sl)] + [("uT", t_) for t_ in range(g0 // 128, (g0 + gs) // 128)],
                       [psk[bk]])
                return bk

            qkctr = [0]

            def qk_job(bidx, g0, gs, gcol, dst, dkeys, dst2=None):
                bk = fproj(bidx, g0, gs)
                i = qkctr[0]
                qkctr[0] += 1
                b = i % 2
                b3 = i % 2
                z = ps[bk][:, 0:gs]
                act(sqb[b][:, 0:gs], z, AF.Square, [psk[bk]], ["sqb%d" % b])
                cp("act", zc[b][:, 0:gs], z, [psk[bk]], ["zc%d" % b])

                def s1():
                    mm(ps[2][:, 0:gs], bo_bf, sqb[b][:, 0:gs], True, True, ["bo_bf", "sqb%d" % b], [psk[2]])
                    act(rst[b][:, 0:gs], ps[2][:, 0:gs], AF.Ln, [psk[2], "misc"], ["rst"], bias=epsc)
                    act(rst[b][:, 0:gs], rst[b][:, 0:gs], AF.Exp, ["rst"], ["rst"], scale=-0.5)
                    stt("dve", qg[b3][:, 0:gs], zc[b][:, 0:gs], gcol, rst[b][:, 0:gs], ALU.mult, ALU.mult,
                        ["zc%d" % b, "svec", "rst"], ["qg%d" % b3])

                    def s2():
                        mm(ps[3][:, 0:gs], protf, qg[b3][:, 0:gs], True, True, ["cm", "qg%d" % b3], [psk[3]])
                        tt("dve", rb[b][:, 0:gs], ps[3][:, 0:gs], sinT[:, g0:g0 + gs], ALU.mult, [psk[3], "sinT"], ["rb"])
                        tt("pool", qg[b3][:, 0:gs], qg[b3][:, 0:gs], cosT[:, g0:g0 + gs], ALU.mult, ["qg%d" % b3, "cosT"],
                           ["qg%d" % b3])
                        if dst2 is None:
                            tt("pool", dst, qg[b3][:, 0:gs], rb[b][:, 0:gs], ALU.add, ["qg%d" % b3, "rb"], dkeys)
                        else:
                            tt("pool", dst[0:64, :], qg[b3][0:64, 0:gs], rb[b][0:64, 0:gs], ALU.add, ["qg%d" % b3, "rb"], dkeys)
                            tt("pool", dst2[64:128, :], qg[b3][64:128, 0:gs], rb[b][64:128, 0:gs], ALU.add,
                               ["qg%d" % b3, "rb"], dkeys)
                    DF.later(1, s2)
                DF.later(2, s1)

            def hq_job(g0, gs):
                bk = fproj(BHQ, g0, gs)
                cp("act", hqT[:, g0:g0 + gs], ps[bk][:, 0:gs], [psk[bk]], ["hqT"])

            def hf_job(h, g0, gs):
                bk = fproj(BHF, g0, gs)
                act(kkT[:, g0:g0 + gs], ps[bk][:, 0:gs], AF.Exp, [psk[bk]], ["kkT"])
                act(kkT[:, g0:g0 + gs], kkT[:, g0:g0 + gs], AF.Ln, ["kkT", "misc"], ["kkT"], bias=onec)
                act(kkT[:, g0:g0 + gs], kkT[:, g0:g0 + gs], AF.Exp, ["kkT"], ["kkT"], scale=-1.0)
                ts("dve", kkT[:, g0:g0 + gs], kkT[:, g0:g0 + gs], omlT[:, h:h + 1], None, ALU.mult, None,
                   ["kkT", "omlT"], ["kkT"])

            if prefetch_only:
                load_w(0, blocks)
                if not own and H > 1:
                    load_w(1, tblk)
                return
            for h in range(H):
                cur_h[0] = h
                tb0 = 384 if (own or h % 2 == 1) else 0
                dma("sp", S_SETUP, omlrow, lbrow_d[:, :, h * 128:(h + 1) * 128], [], ["omlrow"])
                tt("dve", omlrow[:, 0, :], omlrow[:, 0, :], omlrow[:, 1, :], ALU.subtract, ["omlrow"], ["omlrow"])
                act(omlrow[:, 0, :], omlrow[:, 0, :], AF.Exp, ["omlrow"], ["omlrow"])
                ts("dve", omlrow[:, 0, :], omlrow[:, 0, :], 1.0, None, ALU.add, None, ["omlrow"], ["omlrow"])
                recip(omlrow[:, 0, :], omlrow[:, 0, :], ["omlrow"], ["omlrow"])
                if own:
                    dma("sp", ssem(), KT[:, 0:TP], KT_s[h], ["KT_s"], [("KT", t_) for t_ in range(NTP)])
                    dma("sp", ssem(), Vaug[:, 0:NTP, :].rearrange("p a b -> p (a b)"), V_s[h], ["V_s"], ["Vaug"])
                cp("dve", Vaug[:, tile0:tile0 + ntile, 128:129],
                   svec[:, SV_VALID + tile0:SV_VALID + tile0 + ntile].unsqueeze(2), ["svec"], ["Vaug"])
                kg_ = svec[:, SV_KG:SV_KG + 1]
                qg_ = svec[:, SV_QG:SV_QG + 1]
                if own:
                    for (g0, gs) in grp:
                        hq_job(g0, gs)
                        DF.tick()
                    if h + 1 < H:
                        load_w(h + 1, [BHQ])

                Sh = S_all[:, h, :]

                def stageA(i):
                    a = i % RA
                    bk = pctr[0] % 2
                    pctr[0] += 1
                    for c in range(NCH):
                        mm(ps[bk][:, 0:ntc], uT[:, c, i * 128:(i + 1) * 128], wbuf[:, c, tb0:tb0 + ntc],
                           c == 0, c == NCH - 1,
                           [("uT", i)] + [("wb", bb) for bb in range(tb0 // 128, (tb0 + ntc) // 128)], [psk[bk]])
                    z = ps[bk]
                    kb_ = [psk[bk]]
                    act(kk_t[a], z[:, 0:128], AF.Exp, kb_, ["kk%d" % a])
                    cp("act", Vaug[:, tile0 + i, 0:128], z[:, 128:256], kb_, ["Vaug"])
                    cp("act", vi_t[a], z[:, 256:384], kb_, ["vi%d" % a])
                    if own:
                        act(gate_t[a], z[:, 384:512], AF.Exp, kb_, ["gate%d" % a], scale=-1.0)
                    act(kk_t[a], kk_t[a], AF.Ln, ["kk%d" % a, "misc"], ["kk%d" % a], bias=onec)
                    act(kk_t[a], kk_t[a], AF.Exp, ["kk%d" % a], ["kk%d" % a], scale=-1.0)
                    if own:
                        act(gate_t[a], gate_t[a], AF.Ln, ["gate%d" % a, "misc"], ["gate%d" % a], bias=onec)
                        act(gate_t[a], gate_t[a], AF.Exp, ["gate%d" % a], ["gate%d" % a], scale=-1.0)
                    tt("dve", kk_t[a], kk_t[a], omlrow[:, 0, :], ALU.mult, ["kk%d" % a, "omlrow"], ["kk%d" % a])
                    early.append(lambda: act(g_t[a], kk_t[a], AF.Ln, ["kk%d" % a, "misc"], ["g%d" % a], scale=-1.0, bias=onec))
                    DF.later(2, lambda: stageC(i))

                def stageC(i):
                    a = i % RA
                    r = i % RC
                    gk = "g%d" % a
                    mm(ps[2][:, 0:128], utf, g_t[a], True, True, ["cm", gk], [psk[2]])
                    mm(ps[3][:, 0:2], g_t[a], cm[:, CM_IND, 0:2], True, True, [gk, "cm"], [psk[3]])
                    if own:
                        mm(ps[4][:, 0:128], g_t[a], dmf, True, True, [gk, "cm"], [psk[4]])
                        mm(ps[4][:, 128:256], g_t[a], ltf, True, True, [gk, "cm"], [psk[4]])
                        tr(ps[4][:, 256:384], kk_t[a], identf, ["kk%d" % a, "cm"], [psk[4]])
                    act(etmp[i % 2], ps[2][:, 0:128], AF.Exp, [psk[2]], ["etmp%d" % (i % 2)])
                    act(dec_t[r], ps[3][:, 0:2], AF.Exp, [psk[3]], ["dec%d" % r])
                    for c in range(2):
                        stt("dve", kdec_t[r][:, c, :], kk_t[a], cm[:, CM_IND, c:c + 1], etmp[i % 2], ALU.mult, ALU.mult,
                            ["kk%d" % a, "cm", "etmp%d" % (i % 2)], ["kdec%d" % r])
                    if own:
                        t0 = i * 128
                        e1 = e1_t[i % 2]
                        act(e1[:, 0, :], ps[4][:, 0:128], AF.Exp, [psk[4]], ["e1_%d" % (i % 2)])
                        act(e1[:, 1, :], ps[4][:, 0:128], AF.Exp, [psk[4]], ["e1_%d" % (i % 2)], scale=-1.0)
                        act(e1[:, 2, :], ps[4][:, 128:256], AF.Exp, [psk[4]], ["e1_%d" % (i % 2)])
                        qi = qin_t[r]
                        tt("dve", qi[:, 0, :], hqT[:, t0:t0 + 128], e1[:, 0, :], ALU.mult, ["hqT", "e1_%d" % (i % 2)], ["qin%d" % r])
                        tt("dve", qi[:, 1, :], ps[4][:, 256:384], e1[:, 1, :], ALU.mult, [psk[4], "e1_%d" % (i % 2)], ["qin%d" % r])
                        tt("pool", qi[:, 2, :], hqT[:, t0:t0 + 128], e1[:, 2, :], ALU.mult, ["hqT", "e1_%d" % (i % 2)], ["qin%d" % r])
                    DF.later(1, lambda: stageD(i))

                def stageD(i):
                    a = i % RA
                    r = i % RC
                    p2 = i % 2
                    if own:
                        qi = qin_t[r]
                        mm(ps[5][:, 0:128], qi[:, 1, :], qi[:, 0, :], True, True, ["qin%d" % r], [psk[5]])
                    for c in range(2):
                        mm(ps[5][:, 128 + 128 * c:256 + 128 * c], kdec_t[r][:, c, :], vi_t[a],
                           True, True, ["kdec%d" % r, "vi%d" % a], [psk[5]])
                    if own:
                        tt("dve", atm[p2], ps[5][:, 0:128], lt_bf, ALU.mult, [psk[5], "lt_bf"], ["atm%d" % p2])
                    for c in range(2):
                        if own:
                            cp("dve", sbf[p2][:, c, :], Sh, [("S", h)], ["sbf%d" % p2])
                        stt("dve", Sh, Sh, dec_t[r][:, c:c + 1], ps[5][:, 128 + 128 * c:256 + 128 * c], ALU.mult, ALU.add,
                            [("S", h), "dec%d" % r, psk[5]], [("S", h)])
                    if own:
                        DF.later(1, lambda: stageE(i))

                def stageE(i):
                    a = i % RA
                    r = i % RC
                    p2 = i % 2
                    qi = qin_t[r]
                    ob = 6 + p2
                    o_ps = ps[ob][:, 0:128]
                    mm(o_ps, atm[p2], vi_t[a], True, False, ["atm%d" % p2, "vi%d" % a], [psk[ob]])
                    mm(ps[ob][0:64, 0:128], qi[:, 2, 0:64], sbf[p2][:, 0, :], False, True, ["qin%d" % r, "sbf%d" % p2], [psk[ob]])
                    mm(ps[ob][64:128, 0:128], qi[:, 2, 64:128], sbf[p2][:, 1, :], False, True, ["qin%d" % r, "sbf%d" % p2], [psk[ob]])
                    cp("dve", osb[p2], o_ps, [psk[ob]], ["osb%d" % p2])
                    memset("dve", hst[p2][:, 0:1], 0.0, ["hst%d" % p2])
                    P.op("dve", lambda e: e.scalar_tensor_tensor(out=otmp[p2], in0=osb[p2], scalar=1.0, in1=osb[p2], op0=ALU.mult,
                                                                  op1=ALU.mult, accum_out=hst[p2][:, 0:1]),
                         ["osb%d" % p2], ["otmp%d" % p2, "hst%d" % p2])
                    ts("dve", hst[p2][:, 1:2], hst[p2][:, 0:1], 1.0 / 128, EPS, ALU.mult, ALU.add, ["hst%d" % p2], ["hst%d" % p2])
                    act(hst[p2][:, 2:3], hst[p2][:, 1:2], AF.Ln, ["hst%d" % p2], ["hst%d" % p2])
                    act(hst[p2][:, 3:4], hst[p2][:, 2:3], AF.Exp, ["hst%d" % p2], ["hst%d" % p2], scale=-0.5)
                    stt("dve", otmp[p2], osb[p2], hst[p2][:, 3:4], rvec[:, RV_ONORM:RV_ONORM + 128], ALU.mult, ALU.mult,
                        ["osb%d" % p2, "hst%d" % p2, "rvec"], ["otmp%d" % p2])
                    tt("pool", ybt[p2], otmp[p2], gate_t[a], ALU.mult, ["otmp%d" % p2, "gate%d" % a], ["ybt%d" % p2])
                    DF.later(1, lambda: stageF(i))

                def stageF(i):
                    p2 = i % 2
                    fb_ = 6 + p2
                    pst = ps[fb_][:].bitcast(BF16)
                    tr(pst[:, 512:640], ybt[p2], ident_bf, ["ybt%d" % p2, "ident_bf"], [psk[fb_]])
                    cp("dve", ybT[:, i * 128:(i + 1) * 128], pst[:, 512:640], [psk[fb_]], ["ybT"])

                for step in range(ntile):
                    run_early()
                    stageA(step)
                    DF.tick()
                run_early()
                if own:
                    if h + 1 < H:
                        load_w(h + 1, tblk)
                elif h + 2 < H:
                    load_w(h + 2, tblk)
                jobs = []
                if own:
                    for (g0, gs) in grp:
                        jobs.append(("q", g0, gs))
                for (g0, gs) in grp:
                    jobs.append(("k", g0, gs))
                for (kind, g0, gs) in jobs:
                    if kind == "k":
                        qk_job(BK, g0, gs, kg_, KT[:, tok0 + g0:tok0 + g0 + gs], [("KT", t_) for t_ in range((tok0 + g0) // 128, (tok0 + g0 + gs) // 128)])
                    else:
                        qk_job(BQ, g0, gs, qg_, QTm[0][:, g0:g0 + gs], [("QT", t_) for t_ in range(g0 // 128, (g0 + gs) // 128)],
                                   dst2=QTm[1][:, g0:g0 + gs])
                    DF.tick()
                if h + 1 < H:
                    load_w(h + 1, [bb for bb in (BQ, BK) if bb in blocks])
                DF.flush()

                if not own:
                    dma("sp", ssem(), KT_s[h], KT[:, 0:TP], [("KT", t_) for t_ in range(NTP)], ["KT_s"])
                    dma("sp", ssem(), V_s[h], Vaug[:, 0:NTP, :].rearrange("p a b -> p (a b)"), ["Vaug"], ["V_s"])
                    continue

                dma("sp", ssem(), YB_s[h], ybT, ["ybT"], ["YB_s"])
                items = []
                for G in range((NTO + 3) // 4):
                    nb = min(4, NTO - 4 * G)
                    nkb = NTP + 4 * G + nb
                    for kb_ in range(nkb):
                        b0 = max(0, kb_ - NTP - 4 * G)
                        for c in range(2):
                            items.append((G, kb_, c, b0, nb, nkb))
                SB = [4, 5, 6]

                def st_item(ii):
                    G, kb_, c, b0, nb, nkb = items[ii]
                    bk = SB[ii % 3]
                    p0 = 64 * c
                    n = (nb - b0) * 128
                    q0 = G * 512 + b0 * 128
                    pk = "PT%d" % (ii % 3)
                    mm(ps[bk][:, 0:n], KT[:, kb_ * 128:(kb_ + 1) * 128], QTm[c][:, q0:q0 + n], True, True,
                       [("KT", kb_)] + [("QT", t_) for t_ in range(q0 // 128, (q0 + n) // 128)], [psk[bk]])
                    act(PT[ii % 3][:, 0:n], ps[bk][:, 0:n], AF.Exp, [psk[bk], "vbias"], [pk], scale=0.125,
                        bias=vbias[:, kb_:kb_ + 1])
                    if kb_ - NTP - 4 * G >= 0:
                        tt("pool", PT[ii % 3][:, 0:128], PT[ii % 3][:, 0:128], mtri_bf, ALU.mult, [pk, "mtri_bf"], [pk])

                def av_item(ii):
                    G, kb_, c, b0, nb, nkb = items[ii]
                    n = (nb - b0) * 128
                    c0 = b0 * 128
                    pk = "PT%d" % (ii % 3)
                    mm(ps[c][:, c0:c0 + n], Vaug[:, kb_, 0:128], PT[ii % 3][:, 0:n], kb_ == 0, kb_ == nkb - 1,
                       ["Vaug", pk], [psk[c]])
                    mm(ps[2 + c][:, c0:c0 + n], ones_bf, PT[ii % 3][:, 0:n], kb_ == 0, kb_ == nkb - 1,
                       ["ones_bf", pk], [psk[2 + c]])

                rl = [zc[0], zc[1]]
                dd = [qg[0], qg[1]]
                rlk = ["zc0", "zc1"]
                ddk = ["qg0", "qg1"]

                def attn_epi(G, nb):
                    n = nb * 128
                    q0 = G * 512
                    act(rl[0][:, 0:n], ps[2][:, 0:n], AF.Ln, [psk[2]], [rlk[0]])
                    act(rl[1][:, 0:n], ps[3][:, 0:n], AF.Ln, [psk[3]], [rlk[1]])
                    act(rl[0][:, 0:n], rl[0][:, 0:n], AF.Exp, [rlk[0]], [rlk[0]], scale=-1.0)
                    act(rl[1][:, 0:n], rl[1][:, 0:n], AF.Exp, [rlk[1]], [rlk[1]], scale=-1.0)
                    tt("dve", dd[0][:, 0:n], ps[0][:, 0:n], rl[0][:, 0:n], ALU.mult, [psk[0], rlk[0]], [ddk[0]])
                    tt("dve", dd[1][:, 0:n], ps[1][:, 0:n], rl[1][:, 0:n], ALU.mult, [psk[1], rlk[1]], [ddk[1]])
                    stt("dve", dd[0][:, 0:n], dd[1][:, 0:n], nlamc, dd[0][:, 0:n], ALU.mult, ALU.add,
                        [ddk[0], ddk[1], "misc"], [ddk[0]])
                    tt("pool", sqb[0][:, 0:n], dd[0][:, 0:n], dd[0][:, 0:n], ALU.mult, [ddk[0]], ["sqb0"])

                    def e1():
                        mm(ps[7][:, 0:n], mean_bf, sqb[0][:, 0:n], True, True, ["mean_bf", "sqb0"], [psk[7]])
                        act(rl[1][:, 0:n], ps[7][:, 0:n], AF.Ln, [psk[7], "misc"], [rlk[1]], bias=epsc)
                        act(rl[1][:, 0:n], rl[1][:, 0:n], AF.Exp, [rlk[1]], [rlk[1]], scale=-0.5)
                        stt("dve", yaT[:, q0:q0 + n], dd[0][:, 0:n], svec[:, SV_SUBLN:SV_SUBLN + 1], rl[1][:, 0:n],
                            ALU.mult, ALU.mult, [ddk[0], "svec", rlk[1]], ["yaT"])
                    DF.later(2, e1)

                nit = len(items)
                for i0 in range(min(2, nit)):
                    st_item(i0)
                for ii in range(nit):
                    if ii + 2 < nit:
                        st_item(ii + 2)
                    av_item(ii)
                    G, kb_, c, b0, nb, nkb = items[ii]
                    if c == 1 and kb_ == nkb - 1:
                        attn_epi(G, nb)
                    DF.tick()
                DF.later(6, lambda h=h: dma("sp", ssem(), YA_s[h], yaT, ["yaT"], ["YA_s"]))
            DF.flush()

        def phase_merge():
            uT, PH0 = uview(TO)
            A = Alloc(PH0, ARENA_WORDS)
            NE = QK // 128
            NW3 = 2 * NCH + 2 * NE
            w3 = [A.get([NW3, 128], BF16) for _ in range(2)]
            yaT = A.get([NE, TO], BF16)
            ybT = A.get([NE, TO], BF16)
            sa = [A.get([512], F32) for _ in range(2)]
            sb_ = [A.get([512], F32) for _ in range(2)]
            mg = [A.get([TO], BF16) for _ in range(2)]
            YA_v = YA_s.rearrange("c p t -> p c t")
            YB_v = YB_s.rearrange("c p t -> p c t")
            nq_ = max(1, NE // 4)
            for q in range(0, NE, nq_):
                dma("sp", ssem(), yaT[:, q:q + nq_, :], YA_v[:, q:q + nq_, :], ["YA_s"], ["yaT3"])
                dma("sp", ssem(), ybT[:, q:q + nq_, :], YB_v[:, q:q + nq_, :], ["YB_s"], ["ybT3"])
            wa_v = w_up_a.rearrange("(c p) e -> p c e", p=128)
            wb_v = w_up_b.rearrange("(c p) e -> p c e", p=128)
            GA = 7 * QK
            GB = 7 * QK + D

            def load3(j):
                b = j % 2
                k = "w3_%d" % b
                dma("pool", wsem(), w3[b][:, 0:NCH, :], w_in_v[:, :, GA + j * 128:GA + (j + 1) * 128], [], [(k, 0)])
                dma("pool", wsem(), w3[b][:, NCH:2 * NCH, :], w_in_v[:, :, GB + j * 128:GB + (j + 1) * 128], [], [(k, 1)])
                dma("pool", wsem(), w3[b][:, 2 * NCH:2 * NCH + NE, :], wa_v[:, :, j * 128:(j + 1) * 128], [], [(k, 2)])
                dma("pool", wsem(), w3[b][:, 2 * NCH + NE:NW3, :], wb_v[:, :, j * 128:(j + 1) * 128], [], [(k, 3)])

            grp = groups(TO)
            utk = [("uT", t_) for t_ in range(NTO)]
            load3(0)
            it = 0
            for j in range(NCH):
                if j + 1 < NCH:
                    load3(j + 1)
                b = j % 2
                k = "w3_%d" % b
                for (g0, gs) in grp:
                    pb = 4 * (it % 2)
                    s2 = it % 2
                    it += 1
                    for c in range(NCH):
                        mm(ps[pb][:, 0:gs], w3[b][:, c, :], uT[:, c, g0:g0 + gs], c == 0, c == NCH - 1,
                           [(k, 0)] + utk, [psk[pb]])
                    for c in range(NCH):
                        mm(ps[pb + 1][:, 0:gs], w3[b][:, NCH + c, :], uT[:, c, g0:g0 + gs], c == 0, c == NCH - 1,
                           [(k, 1)] + utk, [psk[pb + 1]])
                    for c in range(NE):
                        mm(ps[pb + 2][:, 0:gs], w3[b][:, 2 * NCH + c, :], yaT[:, c, g0:g0 + gs], c == 0, c == NE - 1,
                           [(k, 2), "yaT3"], [psk[pb + 2]])
                    for c in range(NE):
                        mm(ps[pb + 3][:, 0:gs], w3[b][:, 2 * NCH + NE + c, :], ybT[:, c, g0:g0 + gs], c == 0, c == NE - 1,
                           [(k, 3), "ybT3"], [psk[pb + 3]])
                    ka, kb_ = "sa%d" % s2, "sb%d" % s2
                    act(sa[s2][:, 0:gs], ps[pb][:, 0:gs], AF.Exp, [psk[pb]], [ka], scale=-1.0)
                    act(sb_[s2][:, 0:gs], ps[pb + 1][:, 0:gs], AF.Exp, [psk[pb + 1]], [kb_], scale=-1.0)
                    act(sa[s2][:, 0:gs], sa[s2][:, 0:gs], AF.Ln, [ka, "misc"], [ka], bias=onec)
                    act(sb_[s2][:, 0:gs], sb_[s2][:, 0:gs], AF.Ln, [kb_, "misc"], [kb_], bias=onec)
                    act(sa[s2][:, 0:gs], sa[s2][:, 0:gs], AF.Exp, [ka], [ka], scale=-1.0)
                    act(sb_[s2][:, 0:gs], sb_[s2][:, 0:gs], AF.Exp, [kb_], [kb_], scale=-1.0)
                    tt("dve", sa[s2][:, 0:gs], sa[s2][:, 0:gs], ps[pb + 2][:, 0:gs], ALU.mult, [ka, psk[pb + 2]], [ka])
                    tt("dve", sb_[s2][:, 0:gs], sb_[s2][:, 0:gs], ps[pb + 3][:, 0:gs], ALU.mult, [kb_, psk[pb + 3]], [kb_])
                    tt("dve", mg[b][:, g0:g0 + gs], sa[s2][:, 0:gs], sb_[s2][:, 0:gs], ALU.add, [ka, kb_], ["mg%d" % b])
                dma("sp", ssem(), MT_s[j], mg[b], ["mg%d" % b], ["MT_s"])

        def phase_out():
            uT, PH0 = uview(TO, late=True)
            A = Alloc(PH0, ARENA_WORDS)
            mT = A.get([NCH, TO], BF16)
            NWU = 7
            ws = WStream([A.get([8, 512], BF16) for _ in range(NWU)], wsem)
            xs = [A.get([512], F32) for _ in range(2)]
            h1 = [A.get([512], F32) for _ in range(2)]
            h1b = [A.get([512], BF16) for _ in range(2)]
            junk = A.get([512], BF16)
            MT_v = MT_s.rearrange("c p t -> p c t")
            nq = max(1, NCH // 8)
            for q in range(0, NCH, nq):
                dma("sp", ssem(), mT[:, q:q + nq, :], MT_v[:, q:q + nq, :], ["MT_s"], ["mT"])
            wo_v = w_out.rearrange("(c p) e -> p c e", p=128)
            nun = (NCH + 7) // 8
            for s in range(NSL):
                for q in range(nun):
                    n = min(8, NCH - q * 8)
                    ws.add(wo_v[:, q * 8:q * 8 + n, s * 512:(s + 1) * 512], n)
            memset("dve", ssq, 0.0, ["ssq"])
            it = 0
            pend_post = [None]
            for s in range(NSL):
                ws.advance(s * nun)
                for t in range(NTO):
                    b = it % 2
                    it += 1
                    pb = b
                    dma("sp", xsem(), xs[b], xloc[(NTP + t) * 128:(NTP + t + 1) * 128, s * 512:(s + 1) * 512], [], ["xs%d" % b])
                    for c in range(NCH):
                        wb_, wk_ = ws.buf(s * nun + c // 8)
                        mm(ps[pb][:, :], mT[:, c, t * 128:(t + 1) * 128], wb_[:, c % 8, :],
                           c == 0, c == NCH - 1, ["mT", wk_], [psk[pb]])
                    tt("dve", h1[b], ps[pb][:, :], xs[b], ALU.add, [psk[pb], "xs%d" % b], ["h1_%d" % b])
                    act(junk, h1[b], AF.Square, ["h1_%d" % b], ["junk", "ssq"], accum=ssq[:, t, s:s + 1])
                    cp("pool", h1b[b], h1[b], ["h1_%d" % b], ["h1b%d" % b])
                    dma("sp", osem(), out[t * 128:(t + 1) * 128, s * 512:(s + 1) * 512], h1[b], ["h1_%d" % b], [("out", t, s)])
                    def post(b=b, s=s, t=t):
                        tb = 2 + b
                        pst = ps[tb][:].bitcast(BF16)
                        for i in range(4):
                            tr(pst[:, i * 128:(i + 1) * 128], h1b[b][:, i * 128:(i + 1) * 128], ident_bf,
                               ["h1b%d" % b, "ident_bf"], [psk[tb]])
                        for i in range(4):
                            c = s * 4 + i
                            gcol = svec[:, SV_GMLP + c:SV_GMLP + c + 1]
                            o_ = uT[:, c, t * 128:(t + 1) * 128]
                            if b == 0:
                                ts("dve", o_, pst[:, i * 128:(i + 1) * 128], gcol, None, ALU.mult, None,
                                   [psk[tb], "svec"], [("vT", t, c)])
                            else:
                                amul(o_, pst[:, i * 128:(i + 1) * 128], gcol, [psk[tb], "svec"], [("vT", t, c)])
                    if pend_post[0] is not None:
                        pend_post[0]()
                    pend_post[0] = post
            if pend_post[0] is not None:
                pend_post[0]()

        def phase_mlp():
            uT, PH0 = uview(TO, late=True)
            A = Alloc(PH0, ARENA_WORDS)
            r2bc = A.get([TO], F32)
            hid = A.get([NFC, TO], BF16)
            rl = [A.get([512], F32) for _ in range(2)]
            rq = [A.get([512], F32) for _ in range(2)]
            stg = [A.get([512], F32) for _ in range(4)]
            rep = A.get([128], F32)
            tmp = A.get([8], F32)
            NWU = min(10, (ARENA_WORDS - A.off) // 2048)
            assert NWU >= 6, NWU
            ws = WStream([A.get([8, 512], BF16) for _ in range(NWU)], w2sem)
            for t in range(NTO):
                P.op("dve", lambda e, t=t: e.reduce_sum(out=tmp[:, 0:1], in_=ssq[:, t, :], axis=AX.X), ["ssq"], ["tmp5"])
                ts("dve", tmp[:, 1:2], tmp[:, 0:1], 1.0 / D, EPS, ALU.mult, ALU.add, ["tmp5"], ["tmp5"])
                recip(tmp[:, 2:3], tmp[:, 1:2], ["tmp5"], ["tmp5"])
                ts("dve", rep, onesf, tmp[:, 2:3], None, ALU.mult, None, ["cm", "tmp5"], ["rep"])
                mm(ps[7][:, 0:128], rep, identf, True, True, ["rep", "cm"], [psk[7]])
                cp("act", r2bc[:, t * 128:(t + 1) * 128], ps[7][:, 0:128], [psk[7]], ["r2bc"])
            w1_v = w_ff1.rearrange("(c p) e -> p c e", p=128)
            w2_v = w_ff2.rearrange("(c p) e -> p c e", p=128)
            nun1 = (NCH + 7) // 8
            nun2 = (NFC + 7) // 8
            grp = groups(TO)
            vtk = [("vT", t_) for t_ in range(NTO)]
            work = []
            for fb in range(NFB):
                for fs in range(FB // 512):
                    col0 = fb * FB + fs * 512
                    u0 = len(ws.specs)
                    for q in range(nun1):
                        n = min(8, NCH - q * 8)
                        ws.add(w1_v[:, q * 8:q * 8 + n, col0:col0 + 512], n)
                    work.append(("f1", fb, fs, u0))
                for s in range(NSL):
                    u0 = len(ws.specs)
                    for q in range(nun2):
                        n = min(8, NFC - q * 8)
                        ws.add(w2_v[:, fb * NFC + q * 8:fb * NFC + q * 8 + n, s * 512:(s + 1) * 512], n)
                    work.append(("f2", fb, s, u0))
            it1 = 0
            it2 = 0
            for w in work:
                u0 = w[3]
                ws.advance(u0)
                if w[0] == "f1":
                    _, fb, fs, _ = w
                    for i in range(4):
                        fc = fs * 4 + i
                        for (g0, gs) in grp:
                            b = it1 % 2
                            it1 += 1
                            pb = b
                            for c in range(NCH):
                                wb_, wk_ = ws.buf(u0 + c // 8)
                                mm(ps[pb][:, 0:gs], wb_[:, c % 8, i * 128:(i + 1) * 128], uT[:, c, g0:g0 + gs],
                                   c == 0, c == NCH - 1, [wk_] + vtk, [psk[pb]])
                            act(rl[b][:, 0:gs], ps[pb][:, 0:gs], AF.Relu, [psk[pb]], ["rl%d" % b])
                            tt("pool", rq[b][:, 0:gs], rl[b][:, 0:gs], rl[b][:, 0:gs], ALU.mult, ["rl%d" % b], ["rq%d" % b])
                            tt("dve", hid[:, fc, g0:g0 + gs], rq[b][:, 0:gs], r2bc[:, g0:g0 + gs], ALU.mult,
                               ["rq%d" % b, "r2bc"], [("hid", fc)])
                else:
                    _, fb, s, _ = w
                    for t in range(NTO):
                        b = it2 % 2
                        sb4 = it2 % 4
                        it2 += 1
                        pb = 2 + b
                        for fc in range(NFC):
                            wb_, wk_ = ws.buf(u0 + fc // 8)
                            mm(ps[pb][:, :], hid[:, fc, t * 128:(t + 1) * 128], wb_[:, fc % 8, :], fc == 0, fc == NFC - 1,
                               [("hid", fc), wk_], [psk[pb]])
                        if b == 0:
                            cp("act", stg[sb4], ps[pb][:, :], [psk[pb]], ["stg%d" % sb4])
                        else:
                            cp("dve", stg[sb4], ps[pb][:, :], [psk[pb]], ["stg%d" % sb4])
                        dma("pool", S_ACC[sb4], out[t * 128:(t + 1) * 128, s * 512:(s + 1) * 512], stg[sb4],
                            ["stg%d" % sb4], [("out", t, s)], accum=True)

        phase_mixer(False, prefetch_only=True)
        phase_norm(0, NTP)
        P.barrier()
        phase_mixer(False)
        P.barrier()
        phase_mixer(True, prefetch_only=True)
        phase_norm(NTP, NTO)
        P.barrier()
        phase_mixer(True)
        P.barrier()
        phase_merge()
        P.barrier()
        phase_out()
        P.barrier()
        phase_mlp()
        P.emit()
    return nc


_CACHE = {}


def _host_inputs(cfg, x, meta_tokens, norm_mix, w_in, da_q_norm, da_k_norm, da_lambda_q1, da_lambda_k1,
                 da_lambda_q2, da_lambda_k2, da_subln, hg_lower_bound, hg_out_norm, w_up_a, w_up_b,
                 w_out, norm_mlp, w_ff1, w_ff2):
    D, H, NCH, NT, TL, SEQ = cfg.D, cfg.H, cfg.NCH, cfg.NT, cfg.TL, cfg.SEQ
    f32 = np.float32
    x = np.asarray(x, f32)
    meta = np.asarray(meta_tokens, f32)
    cmat = _const_mats()
    SV_N = 2 * NCH + 2 + 2 * H + NT + 1
    rvec = np.zeros((128, 512), f32)
    rvec[:, 0:64] = np.asarray(da_lambda_q1, f32).reshape(1, 64)
    rvec[:, 64:128] = np.asarray(da_lambda_q2, f32).reshape(1, 64)
    rvec[:, 128:192] = np.asarray(da_lambda_k1, f32).reshape(1, 64)
    rvec[:, 192:256] = np.asarray(da_lambda_k2, f32).reshape(1, 64)
    rvec[:, 256:384] = np.asarray(da_subln, f32).reshape(1, 128)
    rvec[:, 384:512] = np.asarray(hg_out_norm, f32).reshape(1, 128)
    lb = np.asarray(hg_lower_bound, f32)
    lbrow = np.ascontiguousarray(np.broadcast_to(lb[None], (128, 2, cfg.QK)))
    shared = {
        "w_in": np.ascontiguousarray(np.asarray(w_in, f32)[0]),
        "w_up_a": np.ascontiguousarray(np.asarray(w_up_a, f32)[0]),
        "w_up_b": np.ascontiguousarray(np.asarray(w_up_b, f32)[0]),
        "w_out": np.ascontiguousarray(np.asarray(w_out, f32)[0]),
        "w_ff1": np.ascontiguousarray(np.asarray(w_ff1, f32)[0]),
        "w_ff2": np.ascontiguousarray(np.asarray(w_ff2, f32)[0]),
        "cmat": cmat, "rvec": rvec, "lbrow": lbrow,
    }
    in_maps = []
    half_len = SEQ // 2
    for core in range(8):
        b, half = core // 2, core % 2
        xl = np.zeros((TL, D), f32)
        if half == 1:
            m0 = 112
            xl[m0:m0 + 16] = meta
            xl[128:128 + half_len] = x[b, 0:half_len]
            xl[128 + half_len:] = x[b, half_len:]
        else:
            m0 = 112 + half_len
            xl[m0:m0 + 16] = meta
            xl[m0 + 16:] = x[b, 0:half_len]
        pos = np.maximum(np.arange(TL) - m0, 0).astype(f32)
        valid = (np.arange(TL) >= m0).astype(f32)
        cosT, sinT = _rope_tables(pos)
        svec = np.zeros((128, SV_N), f32)
        svec[:, 0:NCH] = np.asarray(norm_mix, f32).reshape(NCH, 128).T
        svec[:, NCH:2 * NCH] = np.asarray(norm_mlp, f32).reshape(NCH, 128).T
        svec[:, 2 * NCH] = np.tile(np.asarray(da_q_norm, f32).reshape(64), 2)
        svec[:, 2 * NCH + 1] = np.tile(np.asarray(da_k_norm, f32).reshape(64), 2)
        svec[:, 2 * NCH + 2:2 * NCH + 2 + H] = lb[0].reshape(H, 128).T
        svec[:, 2 * NCH + 2 + H:2 * NCH + 2 + 2 * H] = lb[1].reshape(H, 128).T
        svec[:, 2 * NCH + 2 + 2 * H:2 * NCH + 2 + 2 * H + NT] = valid.reshape(NT, 128).T
        svec[:, 2 * NCH + 2 + 2 * H + NT] = np.asarray(da_subln, f32).reshape(128)
        m = dict(shared)
        m.update({"xloc": xl, "cosT": cosT, "sinT": sinT, "svec": svec})
        in_maps.append(m)
    return in_maps


def run_cfg(cfg, inputs, trace=False):
    key = (cfg.D, cfg.SEQ, cfg.DFF)
    if key not in _CACHE:
        _CACHE[key] = build_program(cfg)
    nc = _CACHE[key]
    in_maps = _host_inputs(cfg, **inputs)
    res = run_bass_kernel_spmd(nc, in_maps, core_ids=list(range(8)), **({"trace": True} if trace else {}))
    half_len = cfg.SEQ // 2
    outp = np.zeros((cfg.B, cfg.SEQ, cfg.D), np.float32)
    for core in range(8):
        b, half = core // 2, core % 2
        outp[b, half * half_len:(half + 1) * half_len] = res.results[core]["out"]
    return outp, res


def kernel(**inputs):
    cfg = Cfg(4096, 2048, 16384)
    outp, _ = run_cfg(cfg, inputs)
    return outp
```

```python
import contextlib
import math
import numpy as np
import concourse.bass as bass
import concourse.mybir as mybir
from concourse.bass_utils import run_bass_kernel_spmd

F32 = mybir.dt.float32
BF16 = mybir.dt.bfloat16
AF = mybir.ActivationFunctionType
ALU = mybir.AluOpType
AX = mybir.AxisListType

EPS = 1e-6
N_META = 16
ROPE_THETA = 500000.0


class _Op:
    __slots__ = ("eng", "fn", "deps", "is_dma", "sem", "val", "needs_inc", "count")

    def __init__(self, eng, fn, is_dma=False, sem=None):
        self.eng = eng
        self.fn = fn
        self.deps = []
        self.is_dma = is_dma
        self.sem = sem
        self.val = 0
        self.needs_inc = False
        self.count = 0


class Prog:
    CENG = ("pe", "act", "dve", "pool")
    ALLENG = ("pe", "act", "dve", "pool", "sp")

    def __init__(self, nc, n_dma_sems=20):
        self.nc = nc
        self.ops = []
        self.last_w = {}
        self.readers = {}
        self.n_dma_sems = n_dma_sems
        self.dma_last = [None] * n_dma_sems
        self.dma_val = [0] * n_dma_sems
        self.last_on = {}
        self.bar_ops = []
        self.bar_pending = set()

    def barrier(self):
        self.bar_ops = [o for o in self.last_on.values()] + [o for o in self.dma_last if o is not None]
        self.bar_pending = set(self.ALLENG)
        self.last_w = {}
        self.readers = {}

    def _track(self, op, reads, writes):
        deps = {}
        for r in reads:
            w = self.last_w.get(r)
            if w is not None:
                deps[id(w)] = w
            if isinstance(r, str) and r.startswith("ps"):
                for ek, rd in self.readers.get(r, {}).items():
                    if ek != op.eng:
                        deps[id(rd)] = rd
        for wk in writes:
            w = self.last_w.get(wk)
            if w is not None:
                deps[id(w)] = w
            for rd in self.readers.get(wk, {}).values():
                deps[id(rd)] = rd
        if op.eng in self.bar_pending:
            self.bar_pending.discard(op.eng)
            for b in self.bar_ops:
                deps[id(b)] = b
        for r in reads:
            d = self.readers.setdefault(r, {})
            d[("d", op.sem) if op.is_dma else op.eng] = op
        for wk in writes:
            self.last_w[wk] = op
            self.readers[wk] = {}
        deps.pop(id(op), None)
        op.deps = list(deps.values())
        self.ops.append(op)
        self.last_on[op.eng] = op
        return op

    def op(self, eng, fn, reads=(), writes=()):
        return self._track(_Op(eng, fn), reads, writes)

    def dma(self, queue, sem, fn, reads=(), writes=()):
        op = _Op(queue, fn, is_dma=True, sem=sem)
        self._track(op, reads, writes)
        prev = self.dma_last[sem]
        if prev is not None and all(prev is not d for d in op.deps):
            op.deps.append(prev)
        self.dma_last[sem] = op
        self.dma_val[sem] += 16
        op.val = self.dma_val[sem]
        return op

    def emit(self, final_eng="sp"):
        nc = self.nc
        for op in self.ops:
            for d in op.deps:
                if not d.is_dma:
                    if d.eng == "pe" and op.eng == "pe" and not op.is_dma:
                        continue
                    d.needs_inc = True
        counts = {e: 0 for e in self.CENG}
        for op in self.ops:
            if not op.is_dma and op.needs_inc:
                counts[op.eng] += 1
                op.count = counts[op.eng]
        streams = {e: [] for e in self.ALLENG}
        for op in self.ops:
            streams[op.eng].append(op)
        with contextlib.ExitStack() as st:
            esem = {e: st.enter_context(nc.semaphore("es_" + e)) for e in self.CENG}
            dsem = [st.enter_context(nc.semaphore("ds_%d" % i)) for i in range(self.n_dma_sems)]
            block = st.enter_context(nc.Block())

            def run(ename, eng):
                waited = {}
                for op in streams[ename]:
                    need = {}
                    for d in op.deps:
                        if d.is_dma:
                            k = ("d", d.sem)
                            v = d.val
                        else:
                            if d.eng == "pe" and ename == "pe" and not op.is_dma:
                                continue
                            k = ("e", d.eng)
                            v = d.count
                        if need.get(k, 0) < v:
                            need[k] = v
                    for k, v in need.items():
                        if waited.get(k, 0) >= v:
                            continue
                        waited[k] = v
                        s = dsem[k[1]] if k[0] == "d" else esem[k[1]]
                        eng.wait_ge(s, v)
                    ins = op.fn(eng)
                    if op.is_dma:
                        ins.then_inc(dsem[op.sem], 16)
                    elif op.needs_inc:
                        ins.then_inc(esem[op.eng], 1)
                if ename == final_eng:
                    for i in range(self.n_dma_sems):
                        if self.dma_val[i] > 0 and waited.get(("d", i), 0) < self.dma_val[i]:
                            eng.wait_ge(dsem[i], self.dma_val[i])
                    for e in self.CENG:
                        if counts[e] > 0 and waited.get(("e", e), 0) < counts[e]:
                            eng.wait_ge(esem[e], counts[e])

            @block.tensor
            def _(e):
                run("pe", e)

            @block.scalar
            def _(e):
                run("act", e)

            @block.vector
            def _(e):
                run("dve", e)

            @block.gpsimd
            def _(e):
                run("pool", e)

            @block.sync
            def _(e):
                run("sp", e)


class Cfg:
    def __init__(self, D, SEQ, DFF, B=4):
        self.D = D
        self.SEQ = SEQ
        self.DFF = DFF
        self.B = B
        self.H = D // 256
        self.NCH = D // 128
        self.QK = self.H * 128
        self.INW = 7 * self.QK + 2 * D
        self.NTO = SEQ // 2 // 128
        self.NTP = self.NTO + 1
        self.NT = self.NTO + self.NTP
        self.TO = self.NTO * 128
        self.TP = self.NTP * 128
        self.TL = self.NT * 128
        self.FB = min(2048, DFF)
        self.NSL = D // 512


CM_IDENT, CM_PROT, CM_UT, CM_LT, CM_DM, CM_BO, CM_MTRI, CM_ONES, CM_IND, CM_N = range(10)


def _const_mats():
    m = np.zeros((CM_N, 128, 128), np.float32)
    m[CM_IDENT] = np.eye(128, dtype=np.float32)
    for mm in range(128):
        r = mm % 64
        if r < 8:
            m[CM_PROT][mm + 8, mm] = -1.0
        elif r < 16:
            m[CM_PROT][mm - 8, mm] = 1.0
    s = np.arange(128)[:, None]
    t = np.arange(128)[None, :]
    same = (s // 64) == (t // 64)
    m[CM_UT] = (same & (s > t)).astype(np.float32)
    m[CM_LT] = (same & (s <= t)).astype(np.float32)
    ref = (t // 64) * 64 + 31
    m[CM_DM] = m[CM_LT] - (same & (s <= ref)).astype(np.float32)
    m[CM_BO] = (same).astype(np.float32) / 64.0
    m[CM_MTRI] = (s <= t).astype(np.float32)
    m[CM_ONES] = 1.0
    m[CM_IND][:64, 0] = 1.0
    m[CM_IND][64:, 1] = 1.0
    return np.ascontiguousarray(m.transpose(1, 0, 2).reshape(128, CM_N * 128))


def _rope_tables(pos):
    half = 8
    inv_freq = (np.float32(ROPE_THETA) ** (-(np.arange(half, dtype=np.float32) * np.float32(2.0)) / np.float32(16))).astype(np.float32)
    ang = pos.astype(np.float32)[None, :] * inv_freq[:, None]
    c = np.cos(ang).astype(np.float32)
    s = np.sin(ang).astype(np.float32)
    TL = pos.shape[0]
    cosT = np.ones((128, TL), np.float32)
    sinT = np.zeros((128, TL), np.float32)
    for mp in range(2):
        b = mp * 64
        cosT[b:b + 8] = c
        cosT[b + 8:b + 16] = c
        sinT[b:b + 8] = s
        sinT[b + 8:b + 16] = s
    return cosT, sinT


def build_program(cfg):
    D, H, NCH, QK, DFF = cfg.D, cfg.H, cfg.NCH, cfg.QK, cfg.DFF
    NTO, NTP, NT, TO, TP, TL = cfg.NTO, cfg.NTP, cfg.NT, cfg.TO, cfg.TP, cfg.TL
    FB, NSL = cfg.FB, cfg.NSL
    NFC = FB // 128
    NFB = DFF // FB

    nc = bass.Bass("TRN2", target_bir_lowering=False)

    def din(name, shape, dt=F32):
        return nc.dram_tensor(name, list(shape), dt, kind="ExternalInput").ap()

    xloc = din("xloc", [TL, D])
    w_in = din("w_in", [D, cfg.INW])
    w_up_a = din("w_up_a", [QK, D])
    w_up_b = din("w_up_b", [QK, D])
    w_out = din("w_out", [D, D])
    w_ff1 = din("w_ff1", [D, DFF])
    w_ff2 = din("w_ff2", [DFF, D])
    cmat = din("cmat", [128, CM_N * 128])
    cosT_d = din("cosT", [128, TL])
    sinT_d = din("sinT", [128, TL])
    SV_GMIX = 0
    SV_GMLP = NCH
    SV_QG = 2 * NCH
    SV_KG = 2 * NCH + 1
    SV_LB0 = 2 * NCH + 2
    SV_LB1 = SV_LB0 + H
    SV_VALID = SV_LB1 + H
    SV_SUBLN = SV_VALID + NT
    SV_N = SV_SUBLN + 1
    svec_d = din("svec", [128, SV_N])
    RV_LAM = 0
    RV_SUBLN = 256
    RV_ONORM = 384
    RV_N = 512
    rvec_d = din("rvec", [128, RV_N])
    lbrow_d = din("lbrow", [128, 2, QK])
    out = nc.dram_tensor("out", [TO, D], F32, kind="ExternalOutput").ap()

    def dscr(name, shape, dt=BF16):
        return nc.dram_tensor(name, list(shape), dt, kind="Internal").ap()

    KT_s = dscr("KT_s", [H, 128, TP])
    V_s = dscr("V_s", [H, 128, NTP * 130])
    YA_s = dscr("YA_s", [H, 128, TO])
    YB_s = dscr("YB_s", [H, 128, TO])
    MT_s = dscr("MT_s", [NCH, 128, TO])

    P = Prog(nc, n_dma_sems=20)
    S_SETUP = 0
    S_W = [1, 2, 3, 4]
    S_X = [5, 6]
    S_SCR = [7, 8, 9]
    S_OUT = [10, 11]
    S_ACC = [12, 13, 14, 15]
    S_W2 = [16, 17, 18, 19]
    wctr = [0]

    def wsem():
        wctr[0] += 1
        return S_W[wctr[0] % 4]

    w2ctr = [0]

    def w2sem():
        w2ctr[0] += 1
        return S_W2[w2ctr[0] % 4]

    xctr = [0]

    def xsem():
        xctr[0] += 1
        return S_X[xctr[0] % 2]

    sctr = [0]

    def ssem():
        sctr[0] += 1
        return S_SCR[sctr[0] % 3]

    octr = [0]

    def osem():
        octr[0] += 1
        return S_OUT[octr[0] % 2]

    def mm(out_, lhsT, rhs, start, stop, reads, writes):
        return P.op("pe", lambda e: e.matmul(out_, lhsT=lhsT, rhs=rhs, start=start, stop=stop), reads, writes)

    def tr(out_, in_, ident, reads, writes):
        return P.op("pe", lambda e: e.transpose(out=out_, in_=in_, identity=ident), reads, writes)

    def act(out_, in_, func, reads, writes, scale=None, bias=None, accum=None):
        kw = {}
        if scale is not None:
            kw["scale"] = scale
        if bias is not None:
            kw["bias"] = bias
        if accum is not None:
            kw["accum_out"] = accum
        return P.op("act", lambda e: e.activation(out=out_, in_=in_, func=func, **kw), reads, writes)

    def tt(eng, out_, in0, in1, op, reads, writes):
        return P.op(eng, lambda e: e.tensor_tensor(out=out_, in0=in0, in1=in1, op=op), reads, writes)

    def ts(eng, out_, in0, s1, s2, op0, op1, reads, writes):
        if op1 is None:
            return P.op(eng, lambda e: e.tensor_scalar(out=out_, in0=in0, scalar1=s1, scalar2=None, op0=op0), reads, writes)
        return P.op(eng, lambda e: e.tensor_scalar(out=out_, in0=in0, scalar1=s1, scalar2=s2, op0=op0, op1=op1), reads, writes)

    def stt(eng, out_, in0, scalar, in1, op0, op1, reads, writes):
        return P.op(eng, lambda e: e.scalar_tensor_tensor(out=out_, in0=in0, scalar=scalar, in1=in1, op0=op0, op1=op1),
                    reads, writes)

    def cp(eng, out_, in_, reads, writes):
        if eng == "act":
            return act(out_, in_, AF.Copy, reads, writes)
        return P.op(eng, lambda e: e.tensor_copy(out=out_, in_=in_), reads, writes)

    def amul(out_, in_, mul_ap, reads, writes):
        return P.op("act", lambda e: e.mul(out=out_, in_=in_, mul=mul_ap), reads, writes)

    def recip(out_, in_, reads, writes):
        return P.op("dve", lambda e: e.reciprocal(out=out_, in_=in_), reads, writes)

    def memset(eng, ap, val, writes):
        return P.op(eng, lambda e: e.memset(ap, val), (), writes)

    def dma(queue, sem, out_, in_, reads, writes, accum=False):
        if accum:
            return P.dma(queue, sem, lambda e: e.dma_start(out=out_, in_=in_, accum_op=ALU.add), reads, writes)
        return P.dma(queue, sem, lambda e: e.dma_start(out=out_, in_=in_), reads, writes)

    with contextlib.ExitStack() as st:
        ARENA_WORDS = 52992
        arena = st.enter_context(nc.sbuf_tensor("arena", [128, ARENA_WORDS], F32))
        ps = [st.enter_context(nc.psum_tensor("ps%d" % i, [128, 512], F32)) for i in range(8)]
        psk = ["ps%d" % i for i in range(8)]

        class Alloc:
            def __init__(self, base, limit):
                self.off = base
                self.limit = limit

            def get(self, free_shape, dt):
                n = 1
                for s_ in free_shape:
                    n *= s_
                words = n if dt == F32 else (n + 1) // 2
                words = (words + 7) // 8 * 8
                a0 = self.off
                self.off += words
                assert self.off <= self.limit, ("SBUF arena overflow", self.off, self.limit)
                v = arena[:, a0:a0 + words]
                if dt != F32:
                    v = v.bitcast(dt)
                v = v[:, 0:n]
                if len(free_shape) == 2:
                    v = v.rearrange("p (a b) -> p a b", b=free_shape[1])
                elif len(free_shape) == 3:
                    v = v.rearrange("p (a b c) -> p a b c", b=free_shape[1], c=free_shape[2])
                return v

        A0 = Alloc(0, ARENA_WORDS)
        cm = A0.get([CM_N, 128], F32)
        svec = A0.get([SV_N], F32)
        rvec = A0.get([RV_N], F32)
        ident_bf = A0.get([128], BF16)
        bo_bf = A0.get([128], BF16)
        mtri_bf = A0.get([128], BF16)
        lt_bf = A0.get([128], BF16)
        ones_bf = A0.get([128], BF16)
        mean_bf = A0.get([128], BF16)
        vbias = A0.get([NT], F32)
        misc = A0.get([16], F32)
        omlT = A0.get([H], F32)
        ssq = A0.get([NTO, NSL], F32)
        lamtmp = A0.get([2, 64], F32)
        UT_BASE2 = A0.off
        S_all = A0.get([H, 128], F32)
        UT_BASE = A0.off

        def uview(ntok, late=False):
            base = UT_BASE2 if late else UT_BASE
            w = NCH * ntok // 2
            v = arena[:, base:base + w].bitcast(BF16)
            return v.rearrange("p (a b) -> p a b", b=ntok), base + w

        identf = cm[:, CM_IDENT, :]
        protf = cm[:, CM_PROT, :]
        utf = cm[:, CM_UT, :]
        ltf = cm[:, CM_LT, :]
        dmf = cm[:, CM_DM, :]
        onesf = cm[:, CM_ONES, :]
        C_EPS, C_ONE, C_LAM, C_NLAM = 0, 1, 2, 3
        epsc = misc[:, C_EPS:C_EPS + 1]
        onec = misc[:, C_ONE:C_ONE + 1]
        nlamc = misc[:, C_NLAM:C_NLAM + 1]

        dma("sp", S_SETUP, cm.rearrange("p a b -> p (a b)"), cmat, [], ["cm"])
        dma("sp", S_SETUP, svec, svec_d, [], ["svec"])
        dma("sp", S_SETUP, rvec, rvec_d, [], ["rvec"])
        cp("dve", ident_bf, identf, ["cm"], ["ident_bf"])
        cp("dve", bo_bf, cm[:, CM_BO, :], ["cm"], ["bo_bf"])
        cp("dve", mtri_bf, cm[:, CM_MTRI, :], ["cm"], ["mtri_bf"])
        cp("dve", lt_bf, ltf, ["cm"], ["lt_bf"])
        cp("dve", ones_bf, cm[:, CM_ONES, :], ["cm"], ["ones_bf"])
        ts("dve", mean_bf, cm[:, CM_ONES, :], 1.0 / 128, None, ALU.mult, None, ["cm"], ["mean_bf"])
        ts("dve", vbias, svec[:, SV_VALID:SV_VALID + NT], -1.0, 30000.0, ALU.add, ALU.mult, ["svec"], ["vbias"])
        ts("dve", svec[:, SV_SUBLN:SV_SUBLN + 1], svec[:, SV_SUBLN:SV_SUBLN + 1], 0.8, None, ALU.mult, None, ["svec"], ["svec"])
        memset("dve", epsc, EPS, ["misc"])
        memset("dve", onec, 1.0, ["misc"])
        memset("dve", S_all, 0.0, ["S"])
        tt("dve", lamtmp, rvec[:, RV_LAM:RV_LAM + 128].rearrange("p (a b) -> p a b", b=64),
           rvec[:, RV_LAM + 128:RV_LAM + 256].rearrange("p (a b) -> p a b", b=64), ALU.mult, ["rvec"], ["lamtmp"])
        P.op("dve", lambda e: e.reduce_sum(out=misc[:, 4:6], in_=lamtmp, axis=AX.X), ["lamtmp"], ["misc"])
        act(misc[:, 6:8], misc[:, 4:6], AF.Exp, ["misc"], ["misc"])
        tt("dve", misc[:, C_LAM:C_LAM + 1], misc[:, 6:7], misc[:, 7:8], ALU.subtract, ["misc"], ["misc"])
        ts("dve", misc[:, C_LAM:C_LAM + 1], misc[:, C_LAM:C_LAM + 1], 0.2, None, ALU.add, None, ["misc"], ["misc"])
        ts("dve", nlamc, misc[:, C_LAM:C_LAM + 1], -1.0, None, ALU.mult, None, ["misc"], ["misc"])
        ts("dve", rvec[:, RV_SUBLN:RV_SUBLN + 128], rvec[:, RV_SUBLN:RV_SUBLN + 128], 0.8, None, ALU.mult, None,
           ["rvec"], ["rvec"])
        tt("dve", omlT, svec[:, SV_LB0:SV_LB0 + H], svec[:, SV_LB1:SV_LB1 + H], ALU.subtract, ["svec"], ["omlT"])
        act(omlT, omlT, AF.Exp, ["omlT"], ["omlT"])
        ts("dve", omlT, omlT, 1.0, None, ALU.add, None, ["omlT"], ["omlT"])
        recip(omlT, omlT, ["omlT"], ["omlT"])

        w_in_v = w_in.rearrange("(c p) e -> p c e", p=128)

        class WStream:
            def __init__(self, bufs, semfn):
                self.bufs = bufs
                self.n = len(bufs)
                self.specs = []
                self.next = 0
                self.semfn = semfn

            def add(self, src, nchunk):
                self.specs.append((src, nchunk))
                return len(self.specs) - 1

            def advance(self, a):
                while self.next < min(len(self.specs), a + self.n):
                    i = self.next
                    src, nchunk = self.specs[i]
                    u = i % self.n
                    dma("pool", self.semfn(), self.bufs[u][:, 0:nchunk, :], src, [], [("wu", u)])
                    self.next += 1

            def buf(self, i):
                return self.bufs[i % self.n], ("wu", i % self.n)

        def groups(ntok):
            g = []
            o = 0
            while o < ntok:
                s_ = min(512, ntok - o)
                g.append((o, s_))
                o += s_
            return g

        WBUF_WORDS = NCH * 896 // 2

        def phase_norm(tile0, ntile):
            uT, PH0 = uview(ntile * 128)
            A = Alloc(PH0 + WBUF_WORDS, ARENA_WORDS)
            xb = [A.get([D], F32) for _ in range(2)]
            xn = [A.get([D], BF16) for _ in range(2)]
            st1 = A.get([8], F32)
            x4 = [S_X[0], S_X[1], S_OUT[0], S_OUT[1]]
            hD = D // 2
            for ti in range(ntile):
                b = ti % 2
                kx, kn = "xb%d" % b, "xn%d" % b
                r0 = (tile0 + ti) * 128
                dma("sp", x4[2 * b], xb[b][:, 0:hD], xloc[r0:r0 + 128, 0:hD], [], [kx + "a"])
                dma("sp", x4[2 * b + 1], xb[b][:, hD:D], xloc[r0:r0 + 128, hD:D], [], [kx + "b"])
                memset("dve", st1[:, 0:1], 0.0, ["st1"])
                act(xn[b], xb[b], AF.Square, [kx + "a", kx + "b"], [kn, "st1"], accum=st1[:, 0:1])
                ts("dve", st1[:, 1:2], st1[:, 0:1], 1.0 / D, EPS, ALU.mult, ALU.add, ["st1"], ["st1"])
                act(st1[:, 2:3], st1[:, 1:2], AF.Ln, ["st1"], ["st1"])
                act(st1[:, 3:4], st1[:, 2:3], AF.Exp, ["st1"], ["st1"], scale=-0.5)
                ts("dve", xn[b], xb[b], st1[:, 3:4], None, ALU.mult, None, [kx + "a", kx + "b", "st1"], [kn])
                for c8 in range(0, NCH, 8):
                    bk = (c8 // 8) % 2
                    pst = ps[bk][:].bitcast(BF16)
                    n8 = min(8, NCH - c8)
                    for i in range(n8):
                        c = c8 + i
                        tr(pst[:, i * 128:(i + 1) * 128], xn[b][:, c * 128:(c + 1) * 128], ident_bf,
                           [kn, "ident_bf"], [psk[bk]])
                    for i in range(n8):
                        c = c8 + i
                        o_ = uT[:, c, ti * 128:(ti + 1) * 128]
                        gcol = svec[:, SV_GMIX + c:SV_GMIX + c + 1]
                        if bk == 0:
                            ts("dve", o_, pst[:, i * 128:(i + 1) * 128], gcol, None, ALU.mult, None,
                               [psk[bk], "svec"], [("uT", ti, c)])
                        else:
                            amul(o_, pst[:, i * 128:(i + 1) * 128], gcol, [psk[bk], "svec"], [("uT", ti, c)])

        def phase_mixer(own, prefetch_only=False):
            ntile = NTO if own else NTP
            ntok = ntile * 128
            tok0 = TP if own else 0
            tile0 = NTP if own else 0
            uT, PH0 = uview(ntok)
            A = Alloc(PH0, ARENA_WORDS)
            wbuf = A.get([NCH, 896], BF16)
            cosT = A.get([ntok], F32)
            sinT = A.get([ntok], F32)
            KT = A.get([TL], BF16)
            Vaug = A.get([NT, 130], BF16)
            if own:
                QTm = [A.get([TO], BF16) for _ in range(2)]
                hqT = A.get([TO], F32)
                yaT = A.get([TO], BF16)
                ybT = A.get([TO], BF16)
            omlrow = A.get([2, 128], F32)
            sqb = [A.get([512], BF16) for _ in range(2)]
            zc = [A.get([512], F32) for _ in range(2)]
            rst = [A.get([512], F32)] * 2
            qg = [A.get([512], F32) for _ in range(2)]
            rb = [A.get([512], F32)] * 2
            RA, RC = 4, 3
            g_t = [A.get([128], F32) for _ in range(RA)]
            kk_t = [A.get([128], F32) for _ in range(RA)]
            vi_t = [A.get([128], BF16) for _ in range(RA)]
            kdec_t = [A.get([2, 128], BF16) for _ in range(RC)]
            etmp = [A.get([128], F32) for _ in range(2)]
            dec_t = [A.get([2], F32) for _ in range(RC)]
            if own:
                PT = [A.get([512], BF16) for _ in range(3)]
                gate_t = [A.get([128], F32) for _ in range(RA)]
                e1_t = [A.get([3, 128], F32) for _ in range(2)]
                qin_t = [A.get([3, 128], BF16) for _ in range(RC)]
                atm = [A.get([128], BF16) for _ in range(2)]
                sbf = [A.get([2, 128], BF16) for _ in range(2)]
                osb = [A.get([128], F32) for _ in range(2)]
                otmp = [A.get([128], F32) for _ in range(2)]
                ybt = [A.get([128], BF16) for _ in range(2)]
                hst = [A.get([8], F32) for _ in range(2)]

            if not prefetch_only:
                dma("sp", S_SETUP, cosT, cosT_d[:, tok0:tok0 + ntok], [], ["cosT"])
                dma("sp", S_SETUP, sinT, sinT_d[:, tok0:tok0 + ntok], [], ["sinT"])
                memset("pool", Vaug, 0.0, ["Vaug"])
                if own:
                    memset("pool", QTm[0][64:128, :], 0.0, ["QTz"])
                    memset("pool", QTm[1][0:64, :], 0.0, ["QTz"])

            BQ, BK, BHQ, BHF, BV, BHI, BHG = range(7)
            colbase = {BQ: 0, BK: QK, BV: 2 * QK, BHQ: 3 * QK, BHF: 4 * QK, BHI: 5 * QK, BHG: 6 * QK}
            blocks = [BQ, BK, BHQ, BHF, BV, BHI, BHG] if own else [BK, BHF, BV, BHI]
            fonly = [bb for bb in blocks if bb < BHF]
            tblk = [bb for bb in blocks if bb >= BHF]
            ntc = 512 if own else 384
            grp = groups(ntok)
            pctr = [0]

            class Defer:
                def __init__(self):
                    self.q = []
                    self.t = 0

                def later(self, delay, fn):
                    self.q.append((self.t + delay, fn))

                def tick(self):
                    self.t += 1
                    due = [f for (d_, f) in self.q if d_ <= self.t]
                    self.q = [(d_, f) for (d_, f) in self.q if d_ > self.t]
                    for f in due:
                        f()

                def flush(self):
                    while self.q:
                        self.tick()

            DF = Defer()
            in_attn = [False]
            early = []

            def run_early():
                fs = list(early)
                del early[:]
                for f in fs:
                    f()

            cur_h = [0]

            def slot(bidx, h):
                if own:
                    return bidx
                if bidx == BK:
                    return 6
                return {BHF: 0, BV: 1, BHI: 2}[bidx] + (0 if h % 2 == 0 else 3)

            def load_w(h, which):
                for bidx in which:
                    col = colbase[bidx] + h * 128
                    sl = slot(bidx, h)
                    dma("pool", wsem(), wbuf[:, :, sl * 128:(sl + 1) * 128], w_in_v[:, :, col:col + 128],
                        [], [("wb", sl)])

            def fproj(bidx, g0, gs):
                bk = pctr[0] % 2
                pctr[0] += 1
                sl = slot(bidx, cur_h[0])
                for c in range(NCH):
                    mm(ps[bk][:, 0:gs], wbuf[:, c, sl * 128:(sl + 1) * 128], uT[:, c, g0:g0 + gs],
                       c == 0, c == NCH - 1, [("wb", sl)] + [("uT", t_) for t_ in range(g0 // 128, (g0 + gs) // 128)],
                       [psk[bk]])
                return bk

            qkctr = [0]

            def qk_job(bidx, g0, gs, gcol, dst, dkeys, dst2=None):
                bk = fproj(bidx, g0, gs)
                i = qkctr[0]
                qkctr[0] += 1
                b = i % 2
                b3 = i % 2
                z = ps[bk][:, 0:gs]
                act(sqb[b][:, 0:gs], z, AF.Square, [psk[bk]], ["sqb%d" % b])
                cp("act", zc[b][:, 0:gs], z, [psk[bk]], ["zc%d" % b])

                def s1():
                    mm(ps[2][:, 0:gs], bo_bf, sqb[b][:, 0:gs], True, True, ["bo_bf", "sqb%d" % b], [psk[2]])
                    act(rst[b][:, 0:gs], ps[2][:, 0:gs], AF.Ln, [psk[2], "misc"], ["rst"], bias=epsc)
                    act(rst[b][:, 0:gs], rst[b][:, 0:gs], AF.Exp, ["rst"], ["rst"], scale=-0.5)
                    stt("dve", qg[b3][:, 0:gs], zc[b][:, 0:gs], gcol, rst[b][:, 0:gs], ALU.mult, ALU.mult,
                        ["zc%d" % b, "svec", "rst"], ["qg%d" % b3])

                    def s2():
                        mm(ps[3][:, 0:gs], protf, qg[b3][:, 0:gs], True, True, ["cm", "qg%d" % b3], [psk[3]])
                        tt("dve", rb[b][:, 0:gs], ps[3][:, 0:gs], sinT[:, g0:g0 + gs], ALU.mult, [psk[3], "sinT"], ["rb"])
                        tt("pool", qg[b3][:, 0:gs], qg[b3][:, 0:gs], cosT[:, g0:g0 + gs], ALU.mult, ["qg%d" % b3, "cosT"],
                           ["qg%d" % b3])
                        if dst2 is None:
                            tt("pool", dst, qg[b3][:, 0:gs], rb[b][:, 0:gs], ALU.add, ["qg%d" % b3, "rb"], dkeys)
                        else:
                            tt("pool", dst[0:64, :], qg[b3][0:64, 0:gs], rb[b][0:64, 0:gs], ALU.add, ["qg%d" % b3, "rb"], dkeys)
                            tt("pool", dst2[64:128, :], qg[b3][64:128, 0:gs], rb[b][64:128, 0:gs], ALU.add,
                               ["qg%d" % b3, "rb"], dkeys)
                    DF.later(1, s2)
                DF.later(2, s1)

            def hq_job(g0, gs):
                bk = fproj(BHQ, g0, gs)
                cp("act", hqT[:, g0:g0 + gs], ps[bk][:, 0:gs], [psk[bk]], ["hqT"])

            def hf_job(h, g0, gs):
                bk = fproj(BHF, g0, gs)
                act(kkT[:, g0:g0 + gs], ps[bk][:, 0:gs], AF.Exp, [psk[bk]], ["kkT"])
                act(kkT[:, g0:g0 + gs], kkT[:, g0:g0 + gs], AF.Ln, ["kkT", "misc"], ["kkT"], bias=onec)
                act(kkT[:, g0:g0 + gs], kkT[:, g0:g0 + gs], AF.Exp, ["kkT"], ["kkT"], scale=-1.0)
                ts("dve", kkT[:, g0:g0 + gs], kkT[:, g0:g0 + gs], omlT[:, h:h + 1], None, ALU.mult, None,
                   ["kkT", "omlT"], ["kkT"])

            if prefetch_only:
                load_w(0, blocks)
                if not own and H > 1:
                    load_w(1, tblk)
                return
            for h in range(H):
                cur_h[0] = h
                tb0 = 384 if (own or h % 2 == 1) else 0
                dma("sp", S_SETUP, omlrow, lbrow_d[:, :, h * 128:(h + 1) * 128], [], ["omlrow"])
                tt("dve", omlrow[:, 0, :], omlrow[:, 0, :], omlrow[:, 1, :], ALU.subtract, ["omlrow"], ["omlrow"])
                act(omlrow[:, 0, :], omlrow[:, 0, :], AF.Exp, ["omlrow"], ["omlrow"])
                ts("dve", omlrow[:, 0, :], omlrow[:, 0, :], 1.0, None, ALU.add, None, ["omlrow"], ["omlrow"])
                recip(omlrow[:, 0, :], omlrow[:, 0, :], ["omlrow"], ["omlrow"])
                if own:
                    dma("sp", ssem(), KT[:, 0:TP], KT_s[h], ["KT_s"], [("KT", t_) for t_ in range(NTP)])
                    dma("sp", ssem(), Vaug[:, 0:NTP, :].rearrange("p a b -> p (a b)"), V_s[h], ["V_s"], ["Vaug"])
                cp("dve", Vaug[:, tile0:tile0 + ntile, 128:129],
                   svec[:, SV_VALID + tile0:SV_VALID + tile0 + ntile].unsqueeze(2), ["svec"], ["Vaug"])
                kg_ = svec[:, SV_KG:SV_KG + 1]
                qg_ = svec[:, SV_QG:SV_QG + 1]
                if own and h == 0:
                    for (g0, gs) in grp:
                        hq_job(g0, gs)
                        DF.tick()
                    if h + 1 < H:
                        load_w(h + 1, [BHQ])

                Sh = S_all[:, h, :]

                def stageA(i):
                    a = i % RA
                    bk = pctr[0] % 2
                    pctr[0] += 1
                    for c in range(NCH):
                        mm(ps[bk][:, 0:ntc], uT[:, c, i * 128:(i + 1) * 128], wbuf[:, c, tb0:tb0 + ntc],
                           c == 0, c == NCH - 1,
                           [("uT", i)] + [("wb", bb) for bb in range(tb0 // 128, (tb0 + ntc) // 128)], [psk[bk]])
                    z = ps[bk]
                    kb_ = [psk[bk]]
                    act(kk_t[a], z[:, 0:128], AF.Exp, kb_, ["kk%d" % a])
                    cp("act", Vaug[:, tile0 + i, 0:128], z[:, 128:256], kb_, ["Vaug"])
                    cp("act", vi_t[a], z[:, 256:384], kb_, ["vi%d" % a])
                    if own:
                        act(gate_t[a], z[:, 384:512], AF.Exp, kb_, ["gate%d" % a], scale=-1.0)
                    act(kk_t[a], kk_t[a], AF.Ln, ["kk%d" % a, "misc"], ["kk%d" % a], bias=onec)
                    act(kk_t[a], kk_t[a], AF.Exp, ["kk%d" % a], ["kk%d" % a], scale=-1.0)
                    if own:
                        act(gate_t[a], gate_t[a], AF.Ln, ["gate%d" % a, "misc"], ["gate%d" % a], bias=onec)
                        act(gate_t[a], gate_t[a], AF.Exp, ["gate%d" % a], ["gate%d" % a], scale=-1.0)
                    tt("dve", kk_t[a], kk_t[a], omlrow[:, 0, :], ALU.mult, ["kk%d" % a, "omlrow"], ["kk%d" % a])
                    early.append(lambda: act(g_t[a], kk_t[a], AF.Ln, ["kk%d" % a, "misc"], ["g%d" % a], scale=-1.0, bias=onec))
                    DF.later(2, lambda: stageC(i))

                def stageC(i):
                    a = i % RA
                    r = i % RC
                    gk = "g%d" % a
                    mm(ps[2][:, 0:128], utf, g_t[a], True, True, ["cm", gk], [psk[2]])
                    mm(ps[3][:, 0:2], g_t[a], cm[:, CM_IND, 0:2], True, True, [gk, "cm"], [psk[3]])
                    if own:
                        mm(ps[4][:, 0:128], g_t[a], dmf, True, True, [gk, "cm"], [psk[4]])
                        mm(ps[4][:, 128:256], g_t[a], ltf, True, True, [gk, "cm"], [psk[4]])
                        tr(ps[4][:, 256:384], kk_t[a], identf, ["kk%d" % a, "cm"], [psk[4]])
                    act(etmp[i % 2], ps[2][:, 0:128], AF.Exp, [psk[2]], ["etmp%d" % (i % 2)])
                    act(dec_t[r], ps[3][:, 0:2], AF.Exp, [psk[3]], ["dec%d" % r])
                    for c in range(2):
                        stt("dve", kdec_t[r][:, c, :], kk_t[a], cm[:, CM_IND, c:c + 1], etmp[i % 2], ALU.mult, ALU.mult,
                            ["kk%d" % a, "cm", "etmp%d" % (i % 2)], ["kdec%d" % r])
                    if own:
                        t0 = i * 128
                        e1 = e1_t[i % 2]
                        act(e1[:, 0, :], ps[4][:, 0:128], AF.Exp, [psk[4]], ["e1_%d" % (i % 2)])
                        act(e1[:, 1, :], ps[4][:, 0:128], AF.Exp, [psk[4]], ["e1_%d" % (i % 2)], scale=-1.0)
                        act(e1[:, 2, :], ps[4][:, 128:256], AF.Exp, [psk[4]], ["e1_%d" % (i % 2)])
                        qi = qin_t[r]
                        tt("dve", qi[:, 0, :], hqT[:, t0:t0 + 128], e1[:, 0, :], ALU.mult, ["hqT", "e1_%d" % (i % 2)], ["qin%d" % r])
                        tt("dve", qi[:, 1, :], ps[4][:, 256:384], e1[:, 1, :], ALU.mult, [psk[4], "e1_%d" % (i % 2)], ["qin%d" % r])
                        tt("pool", qi[:, 2, :], hqT[:, t0:t0 + 128], e1[:, 2, :], ALU.mult, ["hqT", "e1_%d" % (i % 2)], ["qin%d" % r])
                    DF.later(1, lambda: stageD(i))

                def stageD(i):
                    a = i % RA
                    r = i % RC
                    p2 = i % 2
                    if own:
                        qi = qin_t[r]
                        mm(ps[5][:, 0:128], qi[:, 1, :], qi[:, 0, :], True, True, ["qin%d" % r], [psk[5]])
                    for c in range(2):
                        mm(ps[5][:, 128 + 128 * c:256 + 128 * c], kdec_t[r][:, c, :], vi_t[a],
                           True, True, ["kdec%d" % r, "vi%d" % a], [psk[5]])
                    if own:
                        tt("dve", atm[p2], ps[5][:, 0:128], lt_bf, ALU.mult, [psk[5], "lt_bf"], ["atm%d" % p2])
                    for c in range(2):
                        if own:
                            cp("dve", sbf[p2][:, c, :], Sh, [("S", h)], ["sbf%d" % p2])
                        stt("dve", Sh, Sh, dec_t[r][:, c:c + 1], ps[5][:, 128 + 128 * c:256 + 128 * c], ALU.mult, ALU.add,
                            [("S", h), "dec%d" % r, psk[5]], [("S", h)])
                    if own:
                        DF.later(1, lambda: stageE(i))

                def stageE(i):
                    a = i % RA
                    r = i % RC
                    p2 = i % 2
                    qi = qin_t[r]
                    ob = 6 + p2
                    o_ps = ps[ob][:, 0:128]
                    mm(o_ps, atm[p2], vi_t[a], True, False, ["atm%d" % p2, "vi%d" % a], [psk[ob]])
                    mm(ps[ob][0:64, 0:128], qi[:, 2, 0:64], sbf[p2][:, 0, :], False, True, ["qin%d" % r, "sbf%d" % p2], [psk[ob]])
                    mm(ps[ob][64:128, 0:128], qi[:, 2, 64:128], sbf[p2][:, 1, :], False, True, ["qin%d" % r, "sbf%d" % p2], [psk[ob]])
                    cp("dve", osb[p2], o_ps, [psk[ob]], ["osb%d" % p2])
                    memset("dve", hst[p2][:, 0:1], 0.0, ["hst%d" % p2])
                    P.op("dve", lambda e: e.scalar_tensor_tensor(out=otmp[p2], in0=osb[p2], scalar=1.0, in1=osb[p2], op0=ALU.mult,
                                                                  op1=ALU.mult, accum_out=hst[p2][:, 0:1]),
                         ["osb%d" % p2], ["otmp%d" % p2, "hst%d" % p2])
                    ts("dve", hst[p2][:, 1:2], hst[p2][:, 0:1], 1.0 / 128, EPS, ALU.mult, ALU.add, ["hst%d" % p2], ["hst%d" % p2])
                    act(hst[p2][:, 2:3], hst[p2][:, 1:2], AF.Ln, ["hst%d" % p2], ["hst%d" % p2])
                    act(hst[p2][:, 3:4], hst[p2][:, 2:3], AF.Exp, ["hst%d" % p2], ["hst%d" % p2], scale=-0.5)
                    stt("dve", otmp[p2], osb[p2], hst[p2][:, 3:4], rvec[:, RV_ONORM:RV_ONORM + 128], ALU.mult, ALU.mult,
                        ["osb%d" % p2, "hst%d" % p2, "rvec"], ["otmp%d" % p2])
                    tt("pool", ybt[p2], otmp[p2], gate_t[a], ALU.mult, ["otmp%d" % p2, "gate%d" % a], ["ybt%d" % p2])
                    DF.later(1, lambda: stageF(i))

                def stageF(i):
                    p2 = i % 2
                    fb_ = 6 + p2
                    pst = ps[fb_][:].bitcast(BF16)
                    tr(pst[:, 512:640], ybt[p2], ident_bf, ["ybt%d" % p2, "ident_bf"], [psk[fb_]])
                    cp("dve", ybT[:, i * 128:(i + 1) * 128], pst[:, 512:640], [psk[fb_]], ["ybT"])

                for step in range(ntile):
                    run_early()
                    stageA(step)
                    DF.tick()
                run_early()
                if own:
                    if h + 1 < H:
                        load_w(h + 1, tblk)
                elif h + 2 < H:
                    load_w(h + 2, tblk)
                jobs = []
                if own:
                    for (g0, gs) in grp:
                        jobs.append(("q", g0, gs))
                for (g0, gs) in grp:
                    jobs.append(("k", g0, gs))
                for (kind, g0, gs) in jobs:
                    if kind == "k":
                        qk_job(BK, g0, gs, kg_, KT[:, tok0 + g0:tok0 + g0 + gs], [("KT", t_) for t_ in range((tok0 + g0) // 128, (tok0 + g0 + gs) // 128)])
                    else:
                        qk_job(BQ, g0, gs, qg_, QTm[0][:, g0:g0 + gs], [("QT", t_) for t_ in range(g0 // 128, (g0 + gs) // 128)],
                                   dst2=QTm[1][:, g0:g0 + gs])
                    DF.tick()
                if h + 1 < H:
                    load_w(h + 1, [bb for bb in (BQ, BK) if bb in blocks])
                if own and h + 1 < H:
                    for (g0, gs) in grp:
                        hq_job(g0, gs)
                        DF.tick()
                    if h + 2 < H:
                        load_w(h + 2, [BHQ])
                DF.flush()

                if not own:
                    dma("sp", ssem(), KT_s[h], KT[:, 0:TP], [("KT", t_) for t_ in range(NTP)], ["KT_s"])
                    dma("sp", ssem(), V_s[h], Vaug[:, 0:NTP, :].rearrange("p a b -> p (a b)"), ["Vaug"], ["V_s"])
                    continue

                dma("sp", ssem(), YB_s[h], ybT, ["ybT"], ["YB_s"])
                items = []
                for G in range((NTO + 3) // 4):
                    nb = min(4, NTO - 4 * G)
                    nkb = NTP + 4 * G + nb
                    for kb_ in range(nkb):
                        b0 = max(0, kb_ - NTP - 4 * G)
                        for c in range(2):
                            items.append((G, kb_, c, b0, nb, nkb))
                SB = [4, 5, 6]

                def st_item(ii):
                    G, kb_, c, b0, nb, nkb = items[ii]
                    bk = SB[ii % 3]
                    p0 = 64 * c
                    n = (nb - b0) * 128
                    q0 = G * 512 + b0 * 128
                    pk = "PT%d" % (ii % 3)
                    mm(ps[bk][:, 0:n], KT[:, kb_ * 128:(kb_ + 1) * 128], QTm[c][:, q0:q0 + n], True, True,
                       [("KT", kb_)] + [("QT", t_) for t_ in range(q0 // 128, (q0 + n) // 128)], [psk[bk]])
                    act(PT[ii % 3][:, 0:n], ps[bk][:, 0:n], AF.Exp, [psk[bk], "vbias"], [pk], scale=0.125,
                        bias=vbias[:, kb_:kb_ + 1])
                    if kb_ - NTP - 4 * G >= 0:
                        tt("dve", PT[ii % 3][:, 0:128], PT[ii % 3][:, 0:128], mtri_bf, ALU.mult, [pk, "mtri_bf"], [pk])

                def av_item(ii):
                    G, kb_, c, b0, nb, nkb = items[ii]
                    n = (nb - b0) * 128
                    c0 = b0 * 128
                    pk = "PT%d" % (ii % 3)
                    mm(ps[c][:, c0:c0 + n], Vaug[:, kb_, 0:128], PT[ii % 3][:, 0:n], kb_ == 0, kb_ == nkb - 1,
                       ["Vaug", pk], [psk[c]])
                    mm(ps[2 + c][:, c0:c0 + n], ones_bf, PT[ii % 3][:, 0:n], kb_ == 0, kb_ == nkb - 1,
                       ["ones_bf", pk], [psk[2 + c]])

                rl = [zc[0], zc[1]]
                dd = [qg[0], qg[1]]
                rlk = ["zc0", "zc1"]
                ddk = ["qg0", "qg1"]

                def attn_epi(G, nb):
                    n = nb * 128
                    q0 = G * 512
                    act(rl[0][:, 0:n], ps[2][:, 0:n], AF.Ln, [psk[2]], [rlk[0]])
                    act(rl[1][:, 0:n], ps[3][:, 0:n], AF.Ln, [psk[3]], [rlk[1]])
                    act(rl[0][:, 0:n], rl[0][:, 0:n], AF.Exp, [rlk[0]], [rlk[0]], scale=-1.0)
                    act(rl[1][:, 0:n], rl[1][:, 0:n], AF.Exp, [rlk[1]], [rlk[1]], scale=-1.0)
                    tt("dve", dd[0][:, 0:n], ps[0][:, 0:n], rl[0][:, 0:n], ALU.mult, [psk[0], rlk[0]], [ddk[0]])
                    tt("dve", dd[1][:, 0:n], ps[1][:, 0:n], rl[1][:, 0:n], ALU.mult, [psk[1], rlk[1]], [ddk[1]])
                    stt("dve", dd[0][:, 0:n], dd[1][:, 0:n], nlamc, dd[0][:, 0:n], ALU.mult, ALU.add,
                        [ddk[0], ddk[1], "misc"], [ddk[0]])
                    tt("pool", sqb[0][:, 0:n], dd[0][:, 0:n], dd[0][:, 0:n], ALU.mult, [ddk[0]], ["sqb0"])

                    def e1():
                        mm(ps[7][:, 0:n], mean_bf, sqb[0][:, 0:n], True, True, ["mean_bf", "sqb0"], [psk[7]])
                        act(rl[1][:, 0:n], ps[7][:, 0:n], AF.Ln, [psk[7], "misc"], [rlk[1]], bias=epsc)
                        act(rl[1][:, 0:n], rl[1][:, 0:n], AF.Exp, [rlk[1]], [rlk[1]], scale=-0.5)
                        stt("dve", yaT[:, q0:q0 + n], dd[0][:, 0:n], svec[:, SV_SUBLN:SV_SUBLN + 1], rl[1][:, 0:n],
                            ALU.mult, ALU.mult, [ddk[0], "svec", rlk[1]], ["yaT"])
                    DF.later(2, e1)

                nit = len(items)
                for i0 in range(min(2, nit)):
                    st_item(i0)
                for ii in range(nit):
                    if ii + 2 < nit:
                        st_item(ii + 2)
                    av_item(ii)
                    G, kb_, c, b0, nb, nkb = items[ii]
                    if c == 1 and kb_ == nkb - 1:
                        attn_epi(G, nb)
                    DF.tick()
                DF.later(6, lambda h=h: dma("sp", ssem(), YA_s[h], yaT, ["yaT"], ["YA_s"]))
            DF.flush()

        def phase_merge():
            uT, PH0 = uview(TO)
            A = Alloc(PH0, ARENA_WORDS)
            NE = QK // 128
            NW3 = 2 * NCH + 2 * NE
            w3 = [A.get([NW3, 128], BF16) for _ in range(2)]
            yaT = A.get([NE, TO], BF16)
            ybT = A.get([NE, TO], BF16)
            sa = [A.get([512], F32) for _ in range(2)]
            sb_ = [A.get([512], F32) for _ in range(2)]
            mg = [A.get([TO], BF16) for _ in range(2)]
            YA_v = YA_s.rearrange("c p t -> p c t")
            YB_v = YB_s.rearrange("c p t -> p c t")
            nq_ = max(1, NE // 4)
            for q in range(0, NE, nq_):
                dma("sp", ssem(), yaT[:, q:q + nq_, :], YA_v[:, q:q + nq_, :], ["YA_s"], ["yaT3"])
                dma("sp", ssem(), ybT[:, q:q + nq_, :], YB_v[:, q:q + nq_, :], ["YB_s"], ["ybT3"])
            wa_v = w_up_a.rearrange("(c p) e -> p c e", p=128)
            wb_v = w_up_b.rearrange("(c p) e -> p c e", p=128)
            GA = 7 * QK
            GB = 7 * QK + D

            def load3(j):
                b = j % 2
                k = "w3_%d" % b
                dma("pool", wsem(), w3[b][:, 0:NCH, :], w_in_v[:, :, GA + j * 128:GA + (j + 1) * 128], [], [(k, 0)])
                dma("pool", wsem(), w3[b][:, NCH:2 * NCH, :], w_in_v[:, :, GB + j * 128:GB + (j + 1) * 128], [], [(k, 1)])
                dma("pool", wsem(), w3[b][:, 2 * NCH:2 * NCH + NE, :], wa_v[:, :, j * 128:(j + 1) * 128], [], [(k, 2)])
                dma("pool", wsem(), w3[b][:, 2 * NCH + NE:NW3, :], wb_v[:, :, j * 128:(j + 1) * 128], [], [(k, 3)])

            grp = groups(TO)
            utk = [("uT", t_) for t_ in range(NTO)]
            load3(0)
            it = 0
            for j in range(NCH):
                if j + 1 < NCH:
                    load3(j + 1)
                b = j % 2
                k = "w3_%d" % b
                for (g0, gs) in grp:
                    pb = 4 * (it % 2)
                    s2 = it % 2
                    it += 1
                    for c in range(NCH):
                        mm(ps[pb][:, 0:gs], w3[b][:, c, :], uT[:, c, g0:g0 + gs], c == 0, c == NCH - 1,
                           [(k, 0)] + utk, [psk[pb]])
                    for c in range(NCH):
                        mm(ps[pb + 1][:, 0:gs], w3[b][:, NCH + c, :], uT[:, c, g0:g0 + gs], c == 0, c == NCH - 1,
                           [(k, 1)] + utk, [psk[pb + 1]])
                    for c in range(NE):
                        mm(ps[pb + 2][:, 0:gs], w3[b][:, 2 * NCH + c, :], yaT[:, c, g0:g0 + gs], c == 0, c == NE - 1,
                           [(k, 2), "yaT3"], [psk[pb + 2]])
                    for c in range(NE):
                        mm(ps[pb + 3][:, 0:gs], w3[b][:, 2 * NCH + NE + c, :], ybT[:, c, g0:g0 + gs], c == 0, c == NE - 1,
                           [(k, 3), "ybT3"], [psk[pb + 3]])
                    ka, kb_ = "sa%d" % s2, "sb%d" % s2
                    act(sa[s2][:, 0:gs], ps[pb][:, 0:gs], AF.Exp, [psk[pb]], [ka], scale=-1.0)
                    act(sb_[s2][:, 0:gs], ps[pb + 1][:, 0:gs], AF.Exp, [psk[pb + 1]], [kb_], scale=-1.0)
                    act(sa[s2][:, 0:gs], sa[s2][:, 0:gs], AF.Ln, [ka, "misc"], [ka], bias=onec)
                    act(sb_[s2][:, 0:gs], sb_[s2][:, 0:gs], AF.Ln, [kb_, "misc"], [kb_], bias=onec)
                    act(sa[s2][:, 0:gs], sa[s2][:, 0:gs], AF.Exp, [ka], [ka], scale=-1.0)
                    act(sb_[s2][:, 0:gs], sb_[s2][:, 0:gs], AF.Exp, [kb_], [kb_], scale=-1.0)
                    tt("dve", sa[s2][:, 0:gs], sa[s2][:, 0:gs], ps[pb + 2][:, 0:gs], ALU.mult, [ka, psk[pb + 2]], [ka])
                    tt("dve", sb_[s2][:, 0:gs], sb_[s2][:, 0:gs], ps[pb + 3][:, 0:gs], ALU.mult, [kb_, psk[pb + 3]], [kb_])
                    tt("dve", mg[b][:, g0:g0 + gs], sa[s2][:, 0:gs], sb_[s2][:, 0:gs], ALU.add, [ka, kb_], ["mg%d" % b])
                dma("sp", ssem(), MT_s[j], mg[b], ["mg%d" % b], ["MT_s"])

        def phase_out():
            uT, PH0 = uview(TO, late=True)
            A = Alloc(PH0, ARENA_WORDS)
            mT = A.get([NCH, TO], BF16)
            NWU = 7
            ws = WStream([A.get([8, 512], BF16) for _ in range(NWU)], wsem)
            xs = [A.get([512], F32) for _ in range(2)]
            h1 = [A.get([512], F32) for _ in range(2)]
            h1b = [A.get([512], BF16) for _ in range(2)]
            junk = A.get([512], BF16)
            MT_v = MT_s.rearrange("c p t -> p c t")
            nq = max(1, NCH // 8)
            for q in range(0, NCH, nq):
                dma("sp", ssem(), mT[:, q:q + nq, :], MT_v[:, q:q + nq, :], ["MT_s"], ["mT"])
            wo_v = w_out.rearrange("(c p) e -> p c e", p=128)
            nun = (NCH + 7) // 8
            for s in range(NSL):
                for q in range(nun):
                    n = min(8, NCH - q * 8)
                    ws.add(wo_v[:, q * 8:q * 8 + n, s * 512:(s + 1) * 512], n)
            memset("dve", ssq, 0.0, ["ssq"])
            it = 0
            pend_post = [None]
            for s in range(NSL):
                ws.advance(s * nun)
                for t in range(NTO):
                    b = it % 2
                    it += 1
                    pb = b
                    dma("sp", xsem(), xs[b], xloc[(NTP + t) * 128:(NTP + t + 1) * 128, s * 512:(s + 1) * 512], [], ["xs%d" % b])
                    for c in range(NCH):
                        wb_, wk_ = ws.buf(s * nun + c // 8)
                        mm(ps[pb][:, :], mT[:, c, t * 128:(t + 1) * 128], wb_[:, c % 8, :],
                           c == 0, c == NCH - 1, ["mT", wk_], [psk[pb]])
                    tt("dve", h1[b], ps[pb][:, :], xs[b], ALU.add, [psk[pb], "xs%d" % b], ["h1_%d" % b])
                    act(junk, h1[b], AF.Square, ["h1_%d" % b], ["junk", "ssq"], accum=ssq[:, t, s:s + 1])
                    cp("pool", h1b[b], h1[b], ["h1_%d" % b], ["h1b%d" % b])
                    dma("sp", osem(), out[t * 128:(t + 1) * 128, s * 512:(s + 1) * 512], h1[b], ["h1_%d" % b], [("out", t, s)])
                    def post(b=b, s=s, t=t):
                        tb = 2 + b
                        pst = ps[tb][:].bitcast(BF16)
                        for i in range(4):
                            tr(pst[:, i * 128:(i + 1) * 128], h1b[b][:, i * 128:(i + 1) * 128], ident_bf,
                               ["h1b%d" % b, "ident_bf"], [psk[tb]])
                        for i in range(4):
                            c = s * 4 + i
                            gcol = svec[:, SV_GMLP + c:SV_GMLP + c + 1]
                            o_ = uT[:, c, t * 128:(t + 1) * 128]
                            if b == 0:
                                ts("dve", o_, pst[:, i * 128:(i + 1) * 128], gcol, None, ALU.mult, None,
                                   [psk[tb], "svec"], [("vT", t, c)])
                            else:
                                amul(o_, pst[:, i * 128:(i + 1) * 128], gcol, [psk[tb], "svec"], [("vT", t, c)])
                    if pend_post[0] is not None:
                        pend_post[0]()
                    pend_post[0] = post
            if pend_post[0] is not None:
                pend_post[0]()

        def phase_mlp():
            uT, PH0 = uview(TO, late=True)
            A = Alloc(PH0, ARENA_WORDS)
            r2bc = A.get([TO], F32)
            hid = A.get([NFC, TO], BF16)
            rl = [A.get([512], F32) for _ in range(2)]
            rq = [A.get([512], F32) for _ in range(2)]
            stg = [A.get([512], F32) for _ in range(4)]
            rep = A.get([128], F32)
            tmp = A.get([8], F32)
            NWU = min(10, (ARENA_WORDS - A.off) // 2048)
            assert NWU >= 6, NWU
            ws = WStream([A.get([8, 512], BF16) for _ in range(NWU)], w2sem)
            for t in range(NTO):
                P.op("dve", lambda e, t=t: e.reduce_sum(out=tmp[:, 0:1], in_=ssq[:, t, :], axis=AX.X), ["ssq"], ["tmp5"])
                ts("dve", tmp[:, 1:2], tmp[:, 0:1], 1.0 / D, EPS, ALU.mult, ALU.add, ["tmp5"], ["tmp5"])
                recip(tmp[:, 2:3], tmp[:, 1:2], ["tmp5"], ["tmp5"])
                ts("dve", rep, onesf, tmp[:, 2:3], None, ALU.mult, None, ["cm", "tmp5"], ["rep"])
                mm(ps[7][:, 0:128], rep, identf, True, True, ["rep", "cm"], [psk[7]])
                cp("act", r2bc[:, t * 128:(t + 1) * 128], ps[7][:, 0:128], [psk[7]], ["r2bc"])
            w1_v = w_ff1.rearrange("(c p) e -> p c e", p=128)
            w2_v = w_ff2.rearrange("(c p) e -> p c e", p=128)
            nun1 = (NCH + 7) // 8
            nun2 = (NFC + 7) // 8
            grp = groups(TO)
            vtk = [("vT", t_) for t_ in range(NTO)]
            work = []
            for fb in range(NFB):
                for fs in range(FB // 512):
                    col0 = fb * FB + fs * 512
                    u0 = len(ws.specs)
                    for q in range(nun1):
                        n = min(8, NCH - q * 8)
                        ws.add(w1_v[:, q * 8:q * 8 + n, col0:col0 + 512], n)
                    work.append(("f1", fb, fs, u0))
                for s in range(NSL):
                    u0 = len(ws.specs)
                    for q in range(nun2):
                        n = min(8, NFC - q * 8)
                        ws.add(w2_v[:, fb * NFC + q * 8:fb * NFC + q * 8 + n, s * 512:(s + 1) * 512], n)
                    work.append(("f2", fb, s, u0))
            it1 = 0
            it2 = 0
            for w in work:
                u0 = w[3]
                ws.advance(u0)
                if w[0] == "f1":
                    _, fb, fs, _ = w
                    for i in range(4):
                        fc = fs * 4 + i
                        for (g0, gs) in grp:
                            b = it1 % 2
                            it1 += 1
                            pb = b
                            for c in range(NCH):
                                wb_, wk_ = ws.buf(u0 + c // 8)
                                mm(ps[pb][:, 0:gs], wb_[:, c % 8, i * 128:(i + 1) * 128], uT[:, c, g0:g0 + gs],
                                   c == 0, c == NCH - 1, [wk_] + vtk, [psk[pb]])
                            act(rl[b][:, 0:gs], ps[pb][:, 0:gs], AF.Relu, [psk[pb]], ["rl%d" % b])
                            tt("pool", rq[b][:, 0:gs], rl[b][:, 0:gs], rl[b][:, 0:gs], ALU.mult, ["rl%d" % b], ["rq%d" % b])
                            tt("dve", hid[:, fc, g0:g0 + gs], rq[b][:, 0:gs], r2bc[:, g0:g0 + gs], ALU.mult,
                               ["rq%d" % b, "r2bc"], [("hid", fc)])
                else:
                    _, fb, s, _ = w
                    for t in range(NTO):
                        b = it2 % 2
                        sb4 = it2 % 4
                        it2 += 1
                        pb = 2 + b
                        for fc in range(NFC):
                            wb_, wk_ = ws.buf(u0 + fc // 8)
                            mm(ps[pb][:, :], hid[:, fc, t * 128:(t + 1) * 128], wb_[:, fc % 8, :], fc == 0, fc == NFC - 1,
                               [("hid", fc), wk_], [psk[pb]])
                        if b == 0:
                            cp("act", stg[sb4], ps[pb][:, :], [psk[pb]], ["stg%d" % sb4])
                        else:
                            cp("dve", stg[sb4], ps[pb][:, :], [psk[pb]], ["stg%d" % sb4])
                        dma("pool", S_ACC[sb4], out[t * 128:(t + 1) * 128, s * 512:(s + 1) * 512], stg[sb4],
                            ["stg%d" % sb4], [("out", t, s)], accum=True)

        phase_mixer(False, prefetch_only=True)
        phase_norm(0, NTP)
        P.barrier()
        phase_mixer(False)
        P.barrier()
        phase_mixer(True, prefetch_only=True)
        phase_norm(NTP, NTO)
        P.barrier()
        phase_mixer(True)
        P.barrier()
        phase_merge()
        P.barrier()
        phase_out()
        P.barrier()
        phase_mlp()
        P.emit()
    return nc


_CACHE = {}


def _host_inputs(cfg, x, meta_tokens, norm_mix, w_in, da_q_norm, da_k_norm, da_lambda_q1, da_lambda_k1,
                 da_lambda_q2, da_lambda_k2, da_subln, hg_lower_bound, hg_out_norm, w_up_a, w_up_b,
                 w_out, norm_mlp, w_ff1, w_ff2):
    D, H, NCH, NT, TL, SEQ = cfg.D, cfg.H, cfg.NCH, cfg.NT, cfg.TL, cfg.SEQ
    f32 = np.float32
    x = np.asarray(x, f32)
    meta = np.asarray(meta_tokens, f32)
    cmat = _const_mats()
    SV_N = 2 * NCH + 2 + 2 * H + NT + 1
    rvec = np.zeros((128, 512), f32)
    rvec[:, 0:64] = np.asarray(da_lambda_q1, f32).reshape(1, 64)
    rvec[:, 64:128] = np.asarray(da_lambda_q2, f32).reshape(1, 64)
    rvec[:, 128:192] = np.asarray(da_lambda_k1, f32).reshape(1, 64)
    rvec[:, 192:256] = np.asarray(da_lambda_k2, f32).reshape(1, 64)
    rvec[:, 256:384] = np.asarray(da_subln, f32).reshape(1, 128)
    rvec[:, 384:512] = np.asarray(hg_out_norm, f32).reshape(1, 128)
    lb = np.asarray(hg_lower_bound, f32)
    lbrow = np.ascontiguousarray(np.broadcast_to(lb[None], (128, 2, cfg.QK)))
    shared = {
        "w_in": np.ascontiguousarray(np.asarray(w_in, f32)[0]),
        "w_up_a": np.ascontiguousarray(np.asarray(w_up_a, f32)[0]),
        "w_up_b": np.ascontiguousarray(np.asarray(w_up_b, f32)[0]),
        "w_out": np.ascontiguousarray(np.asarray(w_out, f32)[0]),
        "w_ff1": np.ascontiguousarray(np.asarray(w_ff1, f32)[0]),
        "w_ff2": np.ascontiguousarray(np.asarray(w_ff2, f32)[0]),
        "cmat": cmat, "rvec": rvec, "lbrow": lbrow,
    }
    in_maps = []
    half_len = SEQ // 2
    for core in range(8):
        b, half = core // 2, core % 2
        xl = np.zeros((TL, D), f32)
        if half == 1:
            m0 = 112
            xl[m0:m0 + 16] = meta
            xl[128:128 + half_len] = x[b, 0:half_len]
            xl[128 + half_len:] = x[b, half_len:]
        else:
            m0 = 112 + half_len
            xl[m0:m0 + 16] = meta
            xl[m0 + 16:] = x[b, 0:half_len]
        pos = np.maximum(np.arange(TL) - m0, 0).astype(f32)
        valid = (np.arange(TL) >= m0).astype(f32)
        cosT, sinT = _rope_tables(pos)
        svec = np.zeros((128, SV_N), f32)
        svec[:, 0:NCH] = np.asarray(norm_mix, f32).reshape(NCH, 128).T
        svec[:, NCH:2 * NCH] = np.asarray(norm_mlp, f32).reshape(NCH, 128).T
        svec[:, 2 * NCH] = np.tile(np.asarray(da_q_norm, f32).reshape(64), 2)
        svec[:, 2 * NCH + 1] = np.tile(np.asarray(da_k_norm, f32).reshape(64), 2)
        svec[:, 2 * NCH + 2:2 * NCH + 2 + H] = lb[0].reshape(H, 128).T
        svec[:, 2 * NCH + 2 + H:2 * NCH + 2 + 2 * H] = lb[1].reshape(H, 128).T
        svec[:, 2 * NCH + 2 + 2 * H:2 * NCH + 2 + 2 * H + NT] = valid.reshape(NT, 128).T
        svec[:, 2 * NCH + 2 + 2 * H + NT] = np.asarray(da_subln, f32).reshape(128)
        m = dict(shared)
        m.update({"xloc": xl, "cosT": cosT, "sinT": sinT, "svec": svec})
        in_maps.append(m)
    return in_maps


def run_cfg(cfg, inputs, trace=False):
    key = (cfg.D, cfg.SEQ, cfg.DFF)
    if key not in _CACHE:
        _CACHE[key] = build_program(cfg)
    nc = _CACHE[key]
    in_maps = _host_inputs(cfg, **inputs)
    res = run_bass_kernel_spmd(nc, in_maps, core_ids=list(range(8)), **({"trace": True} if trace else {}))
    half_len = cfg.SEQ // 2
    outp = np.zeros((cfg.B, cfg.SEQ, cfg.D), np.float32)
    for core in range(8):
        b, half = core // 2, core % 2
        outp[b, half * half_len:(half + 1) * half_len] = res.results[core]["out"]
    return outp, res


def kernel(**inputs):
    cfg = Cfg(4096, 2048, 16384)
    outp, _ = run_cfg(cfg, inputs)
    return outp
```
